# Optimizing a Trainium2 kernel written in Bass

```python
import jax, jax.numpy as jnp
from jax import lax
import numpy as np

D_MODEL = 1024
BATCH = 8
SEQ = 4096
DEPTH = 2

N_MIXERS = 2
D_TOK = 3 * D_MODEL // 4
D_MEM = D_MODEL - D_TOK
HG_EXPAND = 128
HG_HEADS = D_TOK // HG_EXPAND
HG_VDIM = D_TOK // HG_HEADS
HG_CHUNK = 64
GM_CHUNK = 128
GM_GROUPS = 6
GM_GDIM = D_TOK // GM_GROUPS
MEM_LEN = 256
MEM_HEADS = 4
MEM_HDIM = D_MEM // MEM_HEADS
D_FF = -(-8 * D_MODEL // (3 * 256)) * 256
N_A = (DEPTH + 1) // 2
N_B = DEPTH // 2
EPS = 1e-6

kernel_name = "hybrid_hgrn2_gmlp_memxattn"


def rmsnorm(x, g):
    xf = x.astype(jnp.float32)
    y = xf * lax.rsqrt(jnp.mean(xf * xf, axis=-1, keepdims=True) + EPS)
    return (y * g.astype(jnp.float32)).astype(x.dtype)


def hgrn2_mix(p, lb):
    B, S, _ = p.shape
    n = S // HG_CHUNK
    q, fz, iv, g = jnp.split(p.astype(jnp.float32), 4, axis=-1)
    lbf = lb.astype(jnp.float32)
    log_f = jnp.log(lbf + (1.0 - lbf) * jax.nn.sigmoid(fz))
    k = -jnp.expm1(log_f)

    def to_chunks(t, d):
        return t.reshape(B, n, HG_CHUNK, HG_HEADS, d).transpose(1, 0, 3, 2, 4)

    qc = to_chunks(q, HG_EXPAND)
    kc = to_chunks(k, HG_EXPAND)
    lc = to_chunks(log_f, HG_EXPAND)
    vc = to_chunks(iv, HG_VDIM)
    causal = jnp.tril(jnp.ones((HG_CHUNK, HG_CHUNK), dtype=bool))[None, None, :, :, None]

    def body(state, inp):
        qb, kb, vb, lg = inp
        b = jnp.cumsum(lg, axis=2)
        inter = jnp.einsum('bhtk,bhkv->bhtv', qb * jnp.exp(b), state)
        diff = b[:, :, :, None, :] - b[:, :, None, :, :]
        decay = jnp.where(causal, jnp.exp(jnp.minimum(diff, 0.0)), 0.0)
        scores = jnp.einsum('bhtk,bhsk,bhtsk->bhts', qb, kb, decay)
        intra = jnp.einsum('bhts,bhsv->bhtv', scores, vb)
        b_last = b[:, :, -1:, :]
        new_state = (jnp.exp(b_last[:, :, 0, :])[..., None] * state
                     + jnp.einsum('bhsk,bhsv->bhkv', kb * jnp.exp(b_last - b), vb))
        return new_state, inter + intra

    s0 = jnp.zeros((B, HG_HEADS, HG_EXPAND, HG_VDIM), jnp.float32)
    _, o = lax.scan(body, s0, (qc, kc, vc, lc))
    o = o.transpose(1, 0, 3, 2, 4).reshape(B, S, HG_HEADS, HG_VDIM)
    o = o * lax.rsqrt(jnp.mean(o * o, axis=-1, keepdims=True) + EPS)
    return o.reshape(B, S, D_TOK) * jax.nn.silu(g)


def gmlp_mix(p, ln_g, ln_b, ws, bs):
    B, S, _ = p.shape
    n = S // GM_CHUNK
    z = jax.nn.gelu(p.astype(jnp.float32), approximate=False)
    u, v = jnp.split(z, 2, axis=-1)
    mu = jnp.mean(v, axis=-1, keepdims=True)
    var = jnp.mean(jnp.square(v - mu), axis=-1, keepdims=True)
    v = (v - mu) * lax.rsqrt(var + EPS) * ln_g.astype(jnp.float32) + ln_b.astype(jnp.float32)
    v = v.reshape(B, n, GM_CHUNK, GM_GROUPS, GM_GDIM)
    w = ws.astype(jnp.float32) * jnp.tril(jnp.ones((GM_CHUNK, GM_CHUNK), jnp.float32))[None]
    sv = jnp.einsum('gts,bnsgc->bntgc', w, v) + bs.astype(jnp.float32).T[None, None, :, :, None]
    return u * sv.reshape(B, S, D_TOK)


def mem_attn(qm, mem, g, w_kv):
    B, S, _ = qm.shape
    m = rmsnorm(mem, g)
    kv = m @ w_kv
    k, v = jnp.split(kv, 2, axis=-1)
    k = k.reshape(B, MEM_LEN, MEM_HEADS, MEM_HDIM)
    v = v.reshape(B, MEM_LEN, MEM_HEADS, MEM_HDIM)
    q = qm.reshape(B, S, MEM_HEADS, MEM_HDIM)
    s = jnp.einsum('bshd,bmhd->bhsm', q, k).astype(jnp.float32) * (MEM_HDIM ** -0.5)
    pr = jax.nn.softmax(s, axis=-1)
    o = jnp.einsum('bhsm,bmhd->bshd', pr, v.astype(jnp.float32))
    return o.reshape(B, S, D_MEM)


def swiglu(h, w_in, w_out):
    a = h @ w_in
    gate, up = jnp.split(a, 2, axis=-1)
    return (jax.nn.silu(gate) * up) @ w_out


def setup_inputs(seed: int = 0) -> dict:
    key = jax.random.key(seed)
    ks = jax.random.split(key, 20)
    f32 = jnp.float32

    def nrm(k, shape, s):
        return jax.random.normal(k, shape, f32) * s

    return {
        "x": nrm(ks[0], (BATCH, SEQ, D_MODEL), 1.0),
        "mem": nrm(ks[1], (BATCH, MEM_LEN, D_MODEL), 1.0),
        "mix_norm": 1.0 + nrm(ks[2], (DEPTH, D_MODEL), 0.02),
        "mem_norm": 1.0 + nrm(ks[3], (DEPTH, D_MODEL), 0.02),
        "w_mem_kv": nrm(ks[4], (DEPTH, D_MODEL, 2 * D_MEM), D_MODEL ** -0.5),
        "w_out": nrm(ks[5], (DEPTH, D_TOK + D_MEM, D_MODEL), (D_TOK + D_MEM) ** -0.5),
        "hg_w_in": nrm(ks[6], (N_A, D_MODEL, 4 * D_TOK + D_MEM), D_MODEL ** -0.5),
        "hg_lb": nrm(ks[7], (DEPTH + 1, D_TOK), 0.5),
        "hg_onorm": 1.0 + nrm(ks[8], (N_A, D_TOK), 0.02),
        "gm_w_in": nrm(ks[9], (N_B, D_MODEL, 2 * D_TOK + D_MEM), D_MODEL ** -0.5),
        "gm_ln_g": 1.0 + nrm(ks[10], (N_B, D_TOK), 0.02),
        "gm_ln_b": nrm(ks[11], (N_B, D_TOK), 0.02),
        "gm_ws": nrm(ks[12], (N_B, GM_GROUPS, GM_CHUNK, GM_CHUNK), GM_CHUNK ** -0.5),
        "gm_bs": 1.0 + nrm(ks[13], (N_B, GM_GROUPS, GM_CHUNK), 0.02),
        "ffn_norm": 1.0 + nrm(ks[14], (DEPTH, D_MODEL), 0.02),
        "w_ffn_in": nrm(ks[15], (DEPTH, D_MODEL, 2 * D_FF), D_MODEL ** -0.5),
        "w_ffn_out": nrm(ks[16], (DEPTH, D_FF, D_MODEL), D_FF ** -0.5),
        "final_norm": 1.0 + nrm(ks[17], (D_MODEL,), 0.02),
    }


def reference(x, mem, mix_norm, mem_norm, w_mem_kv, w_out, hg_w_in, hg_lb, hg_onorm,
              gm_w_in, gm_ln_g, gm_ln_b, gm_ws, gm_bs, ffn_norm, w_ffn_in, w_ffn_out,
              final_norm):
    lb_all = jnp.cumsum(jax.nn.softmax(hg_lb.astype(jnp.float32), axis=0), axis=0)
    for i in range(DEPTH):
        h = rmsnorm(x, mix_norm[i])
        j = i // N_MIXERS
        if i % N_MIXERS == 0:
            p = h @ hg_w_in[j]
            tok = hgrn2_mix(p[..., :4 * D_TOK], lb_all[i]) * hg_onorm[j].astype(jnp.float32)
            qm = p[..., 4 * D_TOK:]
        else:
            p = h @ gm_w_in[j]
            tok = gmlp_mix(p[..., :2 * D_TOK], gm_ln_g[j], gm_ln_b[j], gm_ws[j], gm_bs[j])
            qm = p[..., 2 * D_TOK:]
        mo = mem_attn(qm, mem, mem_norm[i], w_mem_kv[i])
        heads = jnp.concatenate([tok, mo], axis=-1).astype(x.dtype)
        x = x + heads @ w_out[i]
        x = x + swiglu(rmsnorm(x, ffn_norm[i]), w_ffn_in[i], w_ffn_out[i])
    return rmsnorm(x, final_norm)
```

```python
import os
import numpy as np
from contextlib import ExitStack
import concourse.bass as bass
import concourse.mybir as mybir
from concourse.bass_utils import run_bass_kernel_spmd

F32 = mybir.dt.float32
BF16 = mybir.dt.bfloat16
AF = mybir.ActivationFunctionType
ALU = mybir.AluOpType
AX = mybir.AxisListType

D = 1024
S = 4096
KD = 8
DTOK = 768
NH = 6
DFF = 2816
NFF = 22
MEM = 256
EPS = 1e-6
T = 512
NC = T // 128
NSLOT = 4
SLOTW = 6144
EPOCH = 3000
DBG_STOP = int(os.environ.get('DBG_STOP', '99'))
DBG_SUB = int(os.environ.get('DBG_SUB', '99'))


class _Op:
    __slots__ = ("eng", "fn", "deps", "dma", "semkey", "inc", "semref", "semval")


class Prog:
    def __init__(self):
        self.ops = []
        self.last_w = {}
        self.readers = {}

    def add(self, eng, fn, reads=(), writes=(), dma=False, semkey=None):
        i = len(self.ops)
        deps = set()
        for k in reads:
            w = self.last_w.get(k)
            if w is not None:
                deps.add(w)
        for k in writes:
            w = self.last_w.get(k)
            if w is not None:
                deps.add(w)
            for r in self.readers.get(k, ()):
                deps.add(r)
        for k in reads:
            self.readers.setdefault(k, []).append(i)
        for k in writes:
            self.last_w[k] = i
            self.readers[k] = []
        deps.discard(i)
        op = _Op()
        op.eng = eng
        op.fn = fn
        op.deps = sorted(deps)
        op.dma = dma
        op.semkey = semkey
        op.inc = False
        op.semref = None
        op.semval = 0
        self.ops.append(op)
        return i

    def emit(self, nc, stack, final_wait_engine="sync"):
        ops = self.ops
        for op in ops:
            for d in op.deps:
                dop = ops[d]
                if dop.eng == "tensor" and op.eng == "tensor" and not dop.dma:
                    continue
                dop.inc = True
        cnt = {}
        dcnt = {}
        semnames = []
        finals = {}
        for op in ops:
            if op.dma:
                name = "d_" + str(op.semkey)
                dcnt[name] = dcnt.get(name, 0) + 16
                op.semref = name
                op.semval = dcnt[name]
                finals[name] = op.semval
            elif op.inc:
                c = cnt.get(op.eng, 0)
                name = "e_%s_%d" % (op.eng, c // EPOCH)
                op.semref = name
                op.semval = (c % EPOCH) + 1
                cnt[op.eng] = c + 1
            else:
                continue
            if name not in semnames:
                semnames.append(name)
        sems = {}
        for name in semnames:
            sems[name] = stack.enter_context(nc.semaphore(name))
        block = stack.enter_context(nc.Block())

        def make(engname):
            def body(eng):
                waited = {}
                for op in ops:
                    if op.eng != engname:
                        continue
                    need = {}
                    for d in op.deps:
                        dop = ops[d]
                        if dop.eng == "tensor" and engname == "tensor" and not dop.dma:
                            continue
                        if dop.semval > need.get(dop.semref, 0):
                            need[dop.semref] = dop.semval
                    for ref, val in need.items():
                        if waited.get(ref, 0) >= val:
                            continue
                        eng.wait_ge(sems[ref], val)
                        waited[ref] = val
                    inst = op.fn(eng)
                    if op.dma:
                        inst.then_inc(sems[op.semref], 16)
                    elif op.inc:
                        inst.then_inc(sems[op.semref], 1)
                if engname == final_wait_engine:
                    for name, val in finals.items():
                        if waited.get(name, 0) < val:
                            eng.wait_ge(sems[name], val)
            return body

        used = set(op.eng for op in ops) | {final_wait_engine}
        for engname in ["sync", "gpsimd", "tensor", "scalar", "vector"]:
            if engname in used:
                getattr(block, engname)(make(engname))


def _lin_block(W, cols):
    kc = W.shape[0] // 128
    sub = W[:, cols]
    return np.ascontiguousarray(sub.reshape(kc, 128, sub.shape[1]).transpose(1, 0, 2))


def _r(a, b):
    return np.arange(a, b)


def _weight_blocks(inp):
    setup = []
    for l in range(2):
        setup.append(("kv%d" % l, _lin_block(inp["w_mem_kv"][l], _r(0, 512)).reshape(128, -1)))
    tile = []
    W = inp["hg_w_in"][0]
    tile.append(("l0_iv", _lin_block(W, _r(1536, 2304)).reshape(128, -1)))
    for h in range(NH):
        blk = np.stack([_lin_block(W, _r(h * 128, h * 128 + 128)),
                        _lin_block(W, _r(768 + h * 128, 768 + h * 128 + 128)),
                        _lin_block(W, _r(2304 + h * 128, 2304 + h * 128 + 128))], axis=1)
        tile.append(("l0_h%d" % h, blk.reshape(128, -1)))
    tile.append(("l0_qm", _lin_block(W, _r(3072, 3328)).reshape(128, -1)))
    G = inp["gm_w_in"][0]
    for l in range(2):
        if l == 1:
            tile.append(("l1_v", _lin_block(G, _r(768, 1536)).reshape(128, -1)))
            tile.append(("l1_u0", _lin_block(G, _r(0, 512)).reshape(128, -1)))
            tile.append(("l1_u1", _lin_block(G, np.concatenate([_r(512, 768), _r(1536, 1792)])).reshape(128, -1)))
        for a in range(2):
            tile.append(("l%d_wo%d" % (l, a), _lin_block(inp["w_out"][l], _r(a * 512, a * 512 + 512)).reshape(128, -1)))
        FI = inp["w_ffn_in"][l]
        for c2 in range(NFF // 2):
            parts = []
            for cc in range(2):
                c = 2 * c2 + cc
                parts.append(np.stack([_lin_block(FI, _r(c * 128, c * 128 + 128)),
                                       _lin_block(FI, _r(DFF + c * 128, DFF + c * 128 + 128))], axis=1))
            blk = np.stack(parts, axis=1)
            tile.append(("l%d_fi%d" % (l, c2), blk.reshape(128, -1)))
        FO = inp["w_ffn_out"][l]
        for j2 in range(4):
            tile.append(("l%d_fo%d" % (l, j2), _lin_block(FO, _r(j2 * 256, j2 * 256 + 256)).reshape(128, -1)))
    return setup, tile


PV = {}
_c = 0
for _n, _w in [("mixn0", 8), ("mixn1", 8), ("ffnn0", 8), ("ffnn1", 8), ("finn", 8), ("memn0", 8), ("memn1", 8),
               ("hglb", 18), ("onorm", 6), ("lng", 6)]:
    PV[_n] = _c
    _c += _w
NPV = _c


def _pack_host(inp):
    inp = {k: np.asarray(v, dtype=np.float32) for k, v in inp.items()}
    setup, tile = _weight_blocks(inp)
    offs = {}
    o = 0
    for n, a in setup + tile:
        offs[n] = (o, a.shape[1])
        o += a.shape[1]
    wall = np.concatenate([a for _, a in setup + tile], axis=1)
    pv = np.zeros((128, NPV), np.float32)

    def put(name, vec):
        k = vec.shape[0] // 128
        pv[:, PV[name]:PV[name] + k] = vec.reshape(k, 128).T

    for l in range(2):
        put("mixn%d" % l, inp["mix_norm"][l])
        put("ffnn%d" % l, inp["ffn_norm"][l])
        put("memn%d" % l, inp["mem_norm"][l])
    put("finn", inp["final_norm"])
    put("hglb", inp["hg_lb"].reshape(-1))
    put("onorm", inp["hg_onorm"][0])
    put("lng", inp["gm_ln_g"][0])
    cst = np.zeros((128, 256), np.float32)
    cst[:, 0:128] = np.eye(128, dtype=np.float32)
    cst[:, 128:256] = np.triu(np.ones((128, 128), np.float32))
    gmc = np.zeros((128, 3 * 768), np.float32)
    gmc[:, 0:768] = inp["gm_ws"][0].transpose(2, 0, 1).reshape(128, 768)
    gmc[:, 768:1536] = np.broadcast_to(inp["gm_ln_b"][0][None, :], (128, 768))
    gmc[:, 1536:2304] = np.broadcast_to(inp["gm_bs"][0].reshape(1, 768), (128, 768))
    shared = {"wall": wall, "pvec": pv, "cst": cst, "gmc": gmc}
    per_core = []
    for b in range(8):
        per_core.append({"xT": np.ascontiguousarray(inp["x"][b].T),
                         "memT": np.ascontiguousarray(inp["mem"][b].T)})
    return shared, per_core, offs, [n for n, _ in setup], [n for n, _ in tile]


def build_program(offs, setup_names, tile_names, ftot, ntiles=S // T, layers=(0, 1), final=True, dbg=None):
    nc = bass.Bass("TRN2", target_bir_lowering=False)
    xT_d = nc.dram_tensor("xT", [D, S], F32, kind="ExternalInput").ap()
    memT_d = nc.dram_tensor("memT", [D, MEM], F32, kind="ExternalInput").ap()
    wall_d = nc.dram_tensor("wall", [128, ftot], F32, kind="ExternalInput").ap()
    pvec_d = nc.dram_tensor("pvec", [128, NPV], F32, kind="ExternalInput").ap()
    cst_d = nc.dram_tensor("cst", [128, 256], F32, kind="ExternalInput").ap()
    gmc_d = nc.dram_tensor("gmc", [128, 2304], F32, kind="ExternalInput").ap()
    outT_d = nc.dram_tensor("outT", [D, S], F32, kind="ExternalOutput").ap()
    xT_v = xT_d.rearrange("(kc p) s -> p kc s", p=128)
    outT_v = outT_d.rearrange("(kc p) s -> p kc s", p=128)
    dbg = dbg or {}
    dbg_d = {}
    for name, shape in dbg.items():
        dbg_d[name] = nc.dram_tensor("dbg_" + name, list(shape), F32, kind="ExternalOutput").ap()

    P = Prog()
    with ExitStack() as st:
        def sb(name, shape, dt):
            return st.enter_context(nc.sbuf_tensor("s_" + name, shape, dt))

        def pst(name, shape, dt):
            return st.enter_context(nc.psum_tensor(name, shape, dt))

        xbufs = [sb("xTa", [128, KD, T], F32), sb("xTb", [128, KD, T], F32)]
        xT = xbufs[0]
        hT = sb("hT", [128, KD, T], BF16)
        lnv = sb("lnv", [128, T], F32)
        rstd = sb("rstd", [128, T], F32)
        rstd2 = rstd
        Vt = sb("Vt", [128, NC, DTOK], BF16)
        hgA = sb("hgA", [128, 12, T], F32)
        L1 = hgA[:, 0:2, :]
        La = hgA[:, 2:4, :]
        bb = hgA[:, 4:6, :]
        E1 = hgA[:, 6:8, :]
        gs = hgA[:, 8:10, :]
        qsb = hgA[:, 10:12, :]
        uT = hgA[:, 0:NH, :]
        UTA = [("L1", 0), ("L1", 1), ("La", 0), ("La", 1), ("bb", 0), ("bb", 1)]
        qt = sb("qt", [128, NH, T], BF16)
        kt = sb("kt", [128, NH, T], BF16)
        sgb = sb("sgb", [128, NH, T], BF16)
        ktok = sb("ktok", [128, NH, T], BF16)
        scT = sb("scT", [128, 6, 128], BF16)
        Sbf = sb("Sbf", [128, 6, 128], BF16)
        onb = sb("onb", [128, 6, 128], BF16)
        svt = sb("svt", [128, 6, 128], F32)
        junk = sb("junk", [128, 6, 128], BF16)
        headsT = sb("headsT", [128, KD, T], BF16)
        qmT = sb("qmT", [128, 2, T], BF16)
        actT = sb("actT", [128, NFF, T], BF16)
        sgt = sb("sgt", [128, 2, T], F32)
        lnd = sgt[:, 0, :]
        rd = sgt[:, 1, :]
        eT = actT[:, 0:4, :]
        sq = actT[:, 0:KD, :]
        SQK = [("actT", c) for c in range(KD)]
        vg = actT[:, 8:20, :].rearrange("p a b -> p (a b)").bitcast(F32).rearrange("p (c f) -> p c f", f=DTOK)

        def vgk(c, half):
            st_ = c * 3072 + half * 1536
            return [("vg", c, half)] + [("actT", 8 + i) for i in range(st_ // 1024, (st_ + 1535) // 1024 + 1)]
        ring = sb("ring", [128, NSLOT, SLOTW], BF16)
        U = sb("U", [128, NH, 128], F32)
        KT = sb("KT", [128, 2, 2, MEM], BF16)
        Vm = sb("Vm", [128, 2, 2, 256], BF16)
        wsb = sb("wsb", [128, NH, 128], BF16)
        wsm = sgt.rearrange("p a b -> p (a b)")[:, 0:768].rearrange("p (g t) -> p g t", t=128)
        gbias = sb("gbias", [128, NH, 128], F32)
        pv = sb("pv", [128, NPV], F32)
        cst = sb("cst", [128, 256], F32)
        identb = sb("identb", [128, 128], BF16)
        trii = sb("trii", [128, 128], mybir.dt.int32)
        trif = cst[:, 128:256]
        onesb = sb("onesb", [128, 128], BF16)
        onesf = sb("onesf", [128, 128], F32)
        sm = sb("sm", [128, 30 + NH * 6 * NC + 18 + 2 * NC + 6 + 2], F32)
        LB, OML, NOML, BLAST, BMIDL = 0, 6, 12, 18, 24
        SM_T = 30
        SM_H = 30
        SM_R = SM_H + NH * 6 * NC
        SM_L = SM_R + 18
        LNOML = SM_L + 2 * NC
        st6 = sb("st6", [128, NC, 12], F32)
        mv = sb("mv", [128, NC, 2], F32)

        psA = [pst("psA%d" % i, [128, 512], F32) for i in range(6)]
        psB = [pst("psB%d" % i, [128, 1024], BF16) for i in range(2)]
        ctr = {"A": 0, "B": 0, "rot": {}}

        def bigps():
            i = ctr["A"] % 6
            ctr["A"] += 1
            return psA[i], ("psA", i)

        def bfps():
            i = ctr["B"] % 2
            ctr["B"] += 1
            return psB[i], ("psB", i)

        def rot(name, n=2):
            i = ctr["rot"].get(name, 0)
            ctr["rot"][name] = i + 1
            return i % n

        def mm_group(out_ap, pairs, reads, writes):
            pairs = list(pairs)

            def fn(e):
                inst = None
                n = len(pairs)
                for i, (l, r) in enumerate(pairs):
                    inst = e.matmul(out_ap, l, r, start=(i == 0), stop=(i == n - 1))
                return inst
            P.add("tensor", fn, reads, writes)

        def act(out, in_, func, reads, writes, **kw):
            P.add("scalar", lambda e: e.activation(out=out, in_=in_, func=func, **kw), reads, writes)

        def dve(fn, reads, writes):
            P.add("vector", fn, reads, writes)

        def dump(name, ap, keys):
            if name in dbg_d:
                P.add("gpsimd", lambda e: e.dma_start(out=dbg_d[name], in_=ap), reads=keys, dma=True, semkey="dbg_" + name)

        seq = list(setup_names) + [n for _ in range(ntiles) for n in tile_names]
        wst = {"issued": 0, "next": 0}

        def issue_weight(i):
            name = seq[i]
            off, w = offs[name]
            slot = i % NSLOT
            b = 1024 if w % 1024 == 0 else 512
            src = wall_d[:, off:off + w].rearrange("p (a b) -> p a b", b=b)
            dst = ring[:, slot, 0:w].rearrange("p (a b) -> p a b", b=b)
            P.add("gpsimd", lambda e: e.dma_start(out=dst, in_=src), writes=[("ring", slot)], dma=True,
                  semkey="ring%d" % slot)

        def get_weight(name):
            i = wst["next"]
            while seq[i] != name:
                i += 1
            wst["next"] = i + 1
            while wst["issued"] < min(len(seq), i + NSLOT):
                issue_weight(wst["issued"])
                wst["issued"] += 1
            slot = i % NSLOT
            return ring[:, slot, :], ("ring", slot)

        P.add("sync", lambda e: e.dma_start(out=pv[:], in_=pvec_d), writes=["pv"], dma=True, semkey="pv")
        P.add("sync", lambda e: e.dma_start(out=cst[:], in_=cst_d), writes=["cst"], dma=True, semkey="cst")
        uTf = uT.rearrange("p a b -> p (a b)")
        P.add("sync", lambda e: e.dma_start(out=uTf[:, 0:2304], in_=gmc_d), writes=["uT"] + UTA, dma=True, semkey="gmc")
        dve(lambda e: e.tensor_copy(out=identb[:], in_=cst[:, 0:128]), ["cst"], ["identb"])
        dve(lambda e: e.memset(onesb[:], 1.0), [], ["onesb"])
        dve(lambda e: e.memset(onesf[:], 1.0), [], ["onesf"])
        dve(lambda e: e.memset(U[:], 0.0), [], [("U", h) for h in range(NH)])
        dve(lambda e: e.memset(sm[:], 0.0), [], ["sm"] + [("blast", h) for h in range(NH)] + [("bmidl", h) for h in range(NH)])
        dve(lambda e: e.memset(scT[:], 0.0), [], [("scT", r, i) for r in range(6) for i in range(2)])
        dve(lambda e: e.tensor_copy(out=trii[:], in_=cst[:, 128:256]), ["cst"], ["trii"])
        lbx = sm[:, SM_T:SM_T + 18]
        act(lbx, pv[:, PV["hglb"]:PV["hglb"] + 18], AF.Exp, ["pv", "sm"], ["lbx"])
        den = sm[:, SM_T + 18:SM_T + 24]
        dve(lambda e: e.tensor_tensor(out=den, in0=lbx[:, 0:6], in1=lbx[:, 6:12], op=ALU.add), ["lbx"], ["den"])
        dve(lambda e: e.tensor_tensor(out=den, in0=den, in1=lbx[:, 12:18], op=ALU.add), ["lbx", "den"], ["den"])
        lnden = sm[:, SM_T + 24:SM_T + 30]
        act(lnden, den, AF.Ln, ["den"], ["lnden"])
        dve(lambda e: e.tensor_tensor(out=lnden, in0=pv[:, PV["hglb"]:PV["hglb"] + 6], in1=lnden, op=ALU.subtract),
            ["pv", "lnden"], ["lnden"])
        act(sm[:, LB:LB + 6], lnden, AF.Exp, ["lnden"], ["lb"])
        dve(lambda e: e.tensor_scalar(out=sm[:, OML:OML + 6], in0=sm[:, LB:LB + 6], scalar1=-1.0, scalar2=1.0,
                                      op0=ALU.mult, op1=ALU.add), ["lb"], ["oml"])
        act(sm[:, LNOML:LNOML + 6], sm[:, OML:OML + 6], AF.Ln, ["oml"], ["lnoml"])
        dve(lambda e: e.tensor_scalar(out=sm[:, NOML:NOML + 6], in0=sm[:, LB:LB + 6], scalar1=-1.0, scalar2=None,
                                      op0=ALU.add), ["lb"], ["noml"])
        for g in range(NH):
            dve(lambda e, g=g: e.tensor_tensor(out=wsm[:, g, :], in0=uTf[:, g * 128:(g + 1) * 128], in1=trif,
                                               op=ALU.mult), ["uT", "cst"], [("wsm", g), ("sgt", g // 4)])
            dve(lambda e, g=g: e.tensor_copy(out=wsb[:, g, :], in_=wsm[:, g, :]), [("wsm", g), ("sgt", g // 4)], [("wsb", g)])
            psb_, pk = bigps()
            ps = psb_[:, 0:128]

            def gb(e, g=g, ps=ps):
                e.matmul(ps, uTf[:, 768 + g * 128:768 + (g + 1) * 128], wsm[:, g, :], start=True, stop=False)
                return e.matmul(ps, onesf[0:1, 0:128], uTf[0:1, 1536 + g * 128:1536 + (g + 1) * 128],
                                start=False, stop=True)
            P.add("tensor", gb, ["uT", ("wsm", g), ("sgt", g // 4), "onesf"], [pk])
            dve(lambda e, g=g, ps=ps: e.tensor_copy(out=gbias[:, g, :], in_=ps), [pk], [("gbias", g)])

        hT_keys = [("hT", kc) for kc in range(KD)]

        xsel = [0]

        def XT():
            return xbufs[xsel[0]]

        def XK(j):
            return ("xT%d" % xsel[0], j)

        def rmsnorm_stats(n=T):
            X = XT()
            for kc in range(KD):
                act(sq[:, kc, 0:n], X[:, kc, 0:n], AF.Square, [XK(kc)], [SQK[kc]])
            ps, pk = bigps()
            mm_group(ps[:, 0:n], [(onesb[:], sq[:, kc, 0:n]) for kc in range(KD)], ["onesb"] + SQK, [pk])
            act(lnv[:, 0:n], ps[:, 0:n], AF.Ln, [pk], ["lnv"], scale=1.0 / D, bias=EPS)
            act(rstd[:, 0:n], lnv[:, 0:n], AF.Exp, ["lnv"], ["rstd"], scale=-0.5)

        def rmsnorm_fm(gname):
            rmsnorm_stats()
            X = XT()
            for kc in range(KD):
                dve(lambda e, kc=kc, X=X: e.scalar_tensor_tensor(out=hT[:, kc, :], in0=X[:, kc, :],
                                                                 scalar=pv[:, PV[gname] + kc:PV[gname] + kc + 1],
                                                                 in1=rstd[:], op0=ALU.mult, op1=ALU.mult),
                    [XK(kc), "pv", "rstd"], [("hT", kc)])

        for l in range(2):
            if l not in layers:
                get_weight("kv%d" % l)
                continue
            P.add("sync", lambda e: e.dma_start(out=xT[:, :, 0:MEM], in_=memT_d.rearrange("(kc p) m -> p kc m", p=128)),
                  writes=[XK(j) for j in range(KD)], dma=True, semkey="xT")
            rmsnorm_stats(MEM)
            gname = "memn%d" % l
            for kc in range(KD):
                dve(lambda e, kc=kc, gname=gname: e.scalar_tensor_tensor(
                    out=hT[:, kc, 0:MEM], in0=xT[:, kc, 0:MEM], scalar=pv[:, PV[gname] + kc:PV[gname] + kc + 1],
                    in1=rstd[:, 0:MEM], op0=ALU.mult, op1=ALU.mult), [XK(kc), "pv", "rstd"], [("hT", kc)])
            wk, wkey = get_weight("kv%d" % l)
            wk3 = wk[:, 0:KD * 512].rearrange("p (k c) -> p k c", c=512)
            for blk in range(2):
                ps, pk = bigps()
                mm_group(ps[:, 0:MEM], [(wk3[:, kc, blk * 128:(blk + 1) * 128], hT[:, kc, 0:MEM]) for kc in range(KD)],
                         [wkey] + hT_keys, [pk])
                dve(lambda e, ps=ps, blk=blk, l=l: e.tensor_copy(out=KT[:, l, blk, :], in_=ps[:, 0:MEM]),
                    [pk], [("KT", l)])
            for mc in range(2):
                ps, pk = bigps()
                mm_group(ps[:, 0:256], [(hT[:, kc, mc * 128:(mc + 1) * 128], wk3[:, kc, 256:512]) for kc in range(KD)],
                         [wkey] + hT_keys, [pk])
                dve(lambda e, ps=ps, mc=mc, l=l: e.tensor_copy(out=Vm[:, l, mc, :], in_=ps[:, 0:256]),
                    [pk], [("Vm", l)])

        def mem_attn_parts(l):
            def scores(pr):
                for hh in range(2):
                    base = hh * 64
                    for mc in range(2):
                        ps, pk = bigps()
                        mm_group(ps[:], [(KT[base:base + 64, l, pr, mc * 128:(mc + 1) * 128], qmT[base:base + 64, pr, :])],
                                 [("KT", l), ("qmT", pr)], [pk])
                        act(eT[:, hh * 2 + mc, :], ps[:], AF.Exp, [pk], [("actT", hh * 2 + mc)], scale=0.125)

            def pv_(pr):
                nps, nk = bigps()
                dps, dk = bigps()
                for hh in range(2):
                    h = pr * 2 + hh
                    mm_group(nps[hh * 64:(hh + 1) * 64, :],
                             [(Vm[:, l, mc, h * 64:(h + 1) * 64], eT[:, hh * 2 + mc, :]) for mc in range(2)],
                             [("Vm", l)] + [("actT", hh * 2 + mc) for mc in range(2)], [nk])
                    mm_group(dps[hh * 64:(hh + 1) * 64, :],
                             [(onesb[:, 0:64], eT[:, hh * 2 + mc, :]) for mc in range(2)],
                             ["onesb"] + [("actT", hh * 2 + mc) for mc in range(2)], [dk])
                act(lnd, dps[:], AF.Ln, [dk], [("sgt", 0)])
                act(rd, lnd, AF.Exp, [("sgt", 0)], [("sgt", 1)], scale=-1.0)
                dve(lambda e, nps=nps, pr=pr: e.tensor_tensor(out=headsT[:, 6 + pr, :], in0=nps[:], in1=rd, op=ALU.mult),
                    [nk, ("sgt", 1)], [("headsT", 6 + pr)])
            return [lambda: scores(0), lambda: pv_(0), lambda: scores(1), lambda: pv_(1)]

        def mem_attn(l):
            for f in mem_attn_parts(l):
                f()

        def out_proj(l):
            for a in range(2):
                w, wkey = get_weight("l%d_wo%d" % (l, a))
                w3 = w[:, 0:KD * 512].rearrange("p (k c) -> p k c", c=512)
                for jj in range(4):
                    j = a * 4 + jj
                    ps, pk = bigps()
                    mm_group(ps[:], [(w3[:, kc, jj * 128:(jj + 1) * 128], headsT[:, kc, :]) for kc in range(KD)],
                             [wkey] + [("headsT", kc) for kc in range(KD)], [pk])
                    dve(lambda e, ps=ps, j=j, X=XT(): e.tensor_tensor(out=X[:, j, :], in0=X[:, j, :], in1=ps[:], op=ALU.add),
                        [pk, XK(j)], [XK(j)])

        def ffn(l, mid=None):
            rmsnorm_fm("ffnn%d" % l)
            for c2 in range(NFF // 2):
                w, wkey = get_weight("l%d_fi%d" % (l, c2))
                w5 = w[:, 0:4096].rearrange("p (a b k c) -> p a b k c", a=2, b=2, k=KD)
                for cc in range(2):
                    c = c2 * 2 + cc
                    gps, gk = bigps()
                    mm_group(gps[:], [(w5[:, cc, 0, kc, :], hT[:, kc, :]) for kc in range(KD)], [wkey] + hT_keys, [gk])
                    ups, uk = bigps()
                    mm_group(ups[:], [(w5[:, cc, 1, kc, :], hT[:, kc, :]) for kc in range(KD)], [wkey] + hT_keys, [uk])
                    r = rot("sgt")
                    act(sgt[:, r, :], gps[:], AF.Silu, [gk], [("sgt", r)])
                    dve(lambda e, r=r, c=c, ups=ups: e.tensor_tensor(out=actT[:, c, :], in0=sgt[:, r, :], in1=ups[:],
                                                                     op=ALU.mult), [("sgt", r), uk], [("actT", c)])
            if mid is not None:
                mid()
            for j2 in range(4):
                w, wkey = get_weight("l%d_fo%d" % (l, j2))
                w3 = w[:, 0:NFF * 256].rearrange("p (k c) -> p k c", c=256)
                for jj in range(2):
                    j = j2 * 2 + jj
                    ps, pk = bigps()
                    mm_group(ps[:], [(w3[:, kc, jj * 128:(jj + 1) * 128], actT[:, kc, :]) for kc in range(NFF)],
                             [wkey] + [("actT", c) for c in range(NFF)], [pk])
                    dve(lambda e, ps=ps, j=j, X=XT(): e.tensor_tensor(out=X[:, j, :], in0=X[:, j, :], in1=ps[:], op=ALU.add),
                        [pk, XK(j)], [XK(j)])

        def hgrn_layer(inter=(), pre_done=False, after_iv=None):
            inter = list(inter)
            if not pre_done:
                rmsnorm_fm("mixn0")
            w, wkey = get_weight("l0_iv")
            w3 = w[:, 0:KD * 768].rearrange("p (k c) -> p k c", c=768)
            for c in range(NC):
                for half in range(2):
                    ps, pk = bigps()
                    mm_group(ps[:, 0:384], [(hT[:, kc, c * 128:(c + 1) * 128], w3[:, kc, half * 384:(half + 1) * 384])
                                            for kc in range(KD)], [wkey] + hT_keys, [pk])
                    act(Vt[:, c, half * 384:(half + 1) * 384], ps[:, 0:384], AF.Copy, [pk], [("Vt", c)])
            if after_iv is not None:
                after_iv()
            qbank = {}

            def front_P(h):
                w, wkey = get_weight("l0_h%d" % h)
                w4 = w[:, 0:3 * KD * 128].rearrange("p (a k c) -> p a k c", a=3, k=KD)
                fps, fk = bigps()
                mm_group(fps[:], [(w4[:, 1, kc, :], hT[:, kc, :]) for kc in range(KD)], [wkey] + hT_keys, [fk])
                gps, gk = bigps()
                mm_group(gps[:], [(w4[:, 2, kc, :], hT[:, kc, :]) for kc in range(KD)], [wkey] + hT_keys, [gk])
                qps, qk = bigps()
                mm_group(qps[:], [(w4[:, 0, kc, :], hT[:, kc, :]) for kc in range(KD)], [wkey] + hT_keys, [qk])
                qbank[h] = (qps, qk, fps, fk, gps, gk)

            def front_A(h):
                p = h % 2
                qps, qk, fps, fk, gps, gk = qbank[h]
                act(L1[:, p, :], fps[:], AF.Exp, [fk], [("L1", p)])
                act(gs[:, p, :], gps[:], AF.Exp, [gk], [("gs", p)], scale=-1.0)
                dve(lambda e, p=p, qps=qps: e.tensor_copy(out=qsb[:, p, :], in_=qps[:]), [qk], [("qsb", p)])
                act(La[:, p, :], L1[:, p, :], AF.Ln, [("L1", p), "lb"], [("La", p)], bias=sm[:, LB + h:LB + h + 1])
                act(L1[:, p, :], L1[:, p, :], AF.Ln, [("L1", p)], [("L1", p)], bias=1.0)
                act(gs[:, p, :], gs[:, p, :], AF.Ln, [("gs", p)], [("gs", p)], bias=1.0)
                act(gs[:, p, :], gs[:, p, :], AF.Exp, [("gs", p)], [("gs", p)], scale=-1.0)
                dve(lambda e, h=h, p=p: e.tensor_tensor_scan(out=bb[:, p, :], data0=La[:, p, :], data1=L1[:, p, :],
                                                             initial=sm[:, BLAST + h:BLAST + h + 1],
                                                             op0=ALU.add, op1=ALU.subtract),
                    [("La", p), ("L1", p), ("blast", h)], [("bb", p)])
                dve(lambda e, h=h, p=p, gps=gps: e.scalar_tensor_tensor(
                    out=sgb[:, h, :], in0=gps[:], scalar=pv[:, PV["onorm"] + h:PV["onorm"] + h + 1], in1=gs[:, p, :],
                    op0=ALU.mult, op1=ALU.mult), [gk, ("gs", p), "pv"], [("sgb", h)])
                bb3 = bb[:, p, :].rearrange("p (c t) -> p c t", t=128)
                bmid = bb3[:, :, 63]
                blc = bb3[:, :, 127]
                hb = SM_H + h * 6 * NC
                nbm = sm[:, hb:hb + NC]
                gin = sm[:, hb + NC:hb + 3 * NC]
                kb = sm[:, hb + 5 * NC:hb + 6 * NC]
                dve(lambda e, nbm=nbm, bmid=bmid: e.tensor_scalar(out=nbm, in0=bmid, scalar1=-1.0, scalar2=None,
                                                                  op0=ALU.mult), [("bb", p)], [("nbm", h)])
                dve(lambda e, kb=kb, bmid=bmid, h=h: e.tensor_scalar(out=kb, in0=bmid, scalar1=sm[:, LNOML + h:LNOML + h + 1],
                                                                     scalar2=None, op0=ALU.add),
                    [("bb", p), "lnoml"], [("kb", h)])
                dve(lambda e, h=h, gin=gin, bmid=bmid: e.tensor_tensor(out=gin[:, 0:1], in0=bmid[:, 0:1],
                                                                       in1=sm[:, BLAST + h:BLAST + h + 1],
                                                                       op=ALU.subtract),
                    [("bb", p), ("blast", h)], [("gin0", h)])
                if NC > 1:
                    dve(lambda e, gin=gin, bmid=bmid, blc=blc: e.tensor_tensor(out=gin[:, 1:NC], in0=bmid[:, 1:NC],
                                                                               in1=blc[:, 0:NC - 1], op=ALU.subtract),
                        [("bb", p)], [("gin1", h)])
                dve(lambda e, gin=gin, bmid=bmid, blc=blc: e.tensor_tensor(out=gin[:, NC:2 * NC], in0=blc, in1=bmid,
                                                                           op=ALU.subtract), [("bb", p)], [("gin2", h)])
                dve(lambda e, h=h, p=p: e.tensor_copy(out=sm[:, BLAST + h:BLAST + h + 1], in_=bb[:, p, T - 1:T]),
                    [("bb", p), ("gin0", h)], [("blast", h)])
                dve(lambda e, p=p: e.tensor_tensor(out=L1[:, p, :], in0=L1[:, p, :], in1=bb[:, p, :], op=ALU.add),
                    [("L1", p), ("bb", p)], [("L1", p)])

            def front_B(h):
                p = h % 2
                bb3 = bb[:, p, :].rearrange("p (c t) -> p c t", t=128)
                bmid = bb3[:, :, 63]
                hb = SM_H + h * 6 * NC
                nbm = sm[:, hb:hb + NC]
                kb = sm[:, hb + 5 * NC:hb + 6 * NC]
                act(sm[:, hb + 3 * NC:hb + 5 * NC], sm[:, hb + NC:hb + 3 * NC], AF.Exp,
                    [("gin0", h), ("gin1", h), ("gin2", h)], [("gam", h)])
                for c in range(NC):
                    cs = slice(c * 128, (c + 1) * 128)
                    act(E1[:, p, cs], bb[:, p, cs], AF.Exp, [("bb", p), ("nbm", h)], [("E1", p, c)], bias=nbm[:, c:c + 1])
                    act(kt[:, h, cs], L1[:, p, cs], AF.Exp, [("L1", p), ("kb", h)], [("kt", h, c)], scale=-1.0,
                        bias=kb[:, c:c + 1])
                dve(lambda e, h=h, p=p: e.tensor_tensor(out=qt[:, h, :], in0=qsb[:, p, :], in1=E1[:, p, :], op=ALU.mult),
                    [("qsb", p)] + [("E1", p, c) for c in range(NC)], [("qt", h)])
                tpsb, tk = bfps()

                def tfn(e, tpsb=tpsb, h=h):
                    inst = None
                    for c in range(NC):
                        inst = e.transpose(tpsb[:, c * 128:(c + 1) * 128], kt[:, h, c * 128:(c + 1) * 128], identb[:])
                    return inst
                P.add("tensor", tfn, [("kt", h, c) for c in range(NC)] + ["identb"], [tk])
                dve(lambda e, tpsb=tpsb, h=h: e.tensor_copy(out=ktok[:, h, :], in_=tpsb[:, 0:T]), [tk], [("ktok", h)])

            front_P(0)
            front_P(1)
            front_A(0)
            for h in range(NH):
                if h + 2 < NH:
                    front_P(h + 2)
                if h + 1 < NH:
                    front_A(h + 1)
                front_B(h)
            w, wkey = get_weight("l0_qm")
            w3 = w[:, 0:KD * 256].rearrange("p (k c) -> p k c", c=256)
            for blk in range(2):
                ps, pk = bigps()
                mm_group(ps[:], [(w3[:, kc, blk * 128:(blk + 1) * 128], hT[:, kc, :]) for kc in range(KD)],
                         [wkey] + hT_keys, [pk])
                act(qmT[:, blk, :], ps[:], AF.Copy, [pk], [("qmT", blk)])

            KTK = lambda h: [("kt", h, c) for c in range(NC)]
            groups = [(c, gi) for c in range(NC) for gi in range(2)]
            gst = {}

            def seg_E(gidx):
                c, gi = groups[gidx]
                c0 = c * 128
                cs = slice(c0, c0 + 128)
                heads = [gi * 3 + j for j in range(3)]
                slot = [(gidx % 2) * 3 + j for j in range(3)]
                spsb, sk = bigps()

                def sfn(e, spsb=spsb, c0=c0, heads=heads):
                    inst = None
                    for j, h in enumerate(heads):
                        o = j * 128
                        e.matmul(spsb[0:64, o:o + 64], kt[:, h, c0:c0 + 64], qt[:, h, c0:c0 + 64], start=True, stop=True)
                        inst = e.matmul(spsb[:, o + 64:o + 128], kt[:, h, c0:c0 + 128], qt[:, h, c0 + 64:c0 + 128],
                                        start=True, stop=True)
                    return inst
                P.add("tensor", sfn, [k for h in heads for k in KTK(h)] + [("qt", h) for h in heads], [sk])
                ppb, pkk = bigps()

                def pfn(e, ppb=ppb, cs=cs, c=c, heads=heads):
                    inst = None
                    for j, h in enumerate(heads):
                        inst = e.matmul(ppb[:, j * 128:(j + 1) * 128], ktok[:, h, cs], Vt[:, c, h * 128:(h + 1) * 128],
                                        start=True, stop=True)
                    return inst
                P.add("tensor", pfn, [("ktok", h) for h in heads] + [("Vt", c)], [pkk])
                for j, h in enumerate(heads):
                    hb = SM_H + h * 6 * NC
                    gam = sm[:, hb + 3 * NC:hb + 5 * NC]
                    act(Sbf[:, slot[j], :], U[:, h, :], AF.Copy, [("U", h), ("gam", h)], [("Sbf", slot[j])],
                        scale=gam[:, c:c + 1])
                s0 = slot[0]
                sp3 = spsb[:, 0:384].rearrange("p (j t) -> p j t", t=128)
                dve(lambda e, sp3=sp3, s0=s0: e.copy_predicated(
                    out=scT[0:64, s0:s0 + 3, 0:64], mask=trii[0:64, 0:64].unsqueeze(1).to_broadcast([64, 3, 64]),
                    data=sp3[0:64, :, 0:64]), [sk, "trii"], [("scT", r, 0) for r in slot])
                dve(lambda e, sp3=sp3, s0=s0: e.copy_predicated(
                    out=scT[:, s0:s0 + 3, 64:128], mask=trii[:, 64:128].unsqueeze(1).to_broadcast([128, 3, 64]),
                    data=sp3[:, :, 64:128]), [sk, "trii"], [("scT", r, 1) for r in slot])
                gst[gidx] = (heads, slot, c, cs)
                return ppb, pkk

            def seg_U(gidx, ppb, pkk):
                heads, slot, c, cs = gst[gidx]
                for j, h in enumerate(heads):
                    hb = SM_H + h * 6 * NC
                    gam = sm[:, hb + 3 * NC:hb + 5 * NC]
                    dve(lambda e, h=h, c=c, j=j, ppb=ppb, gam=gam: e.scalar_tensor_tensor(
                        out=U[:, h, :], in0=U[:, h, :], scalar=gam[:, c:c + 1], in1=ppb[:, j * 128:(j + 1) * 128],
                        op0=ALU.mult, op1=ALU.add), [("U", h), ("gam", h), pkk], [("U", h)])
                for j, h in enumerate(heads):
                    hb = SM_H + h * 6 * NC
                    gam = sm[:, hb + 3 * NC:hb + 5 * NC]
                    act(U[:, h, :], U[:, h, :], AF.Copy, [("U", h), ("gam", h)], [("U", h)],
                        scale=gam[:, NC + c:NC + c + 1])

            def seg_M(gidx):
                heads, slot, c, cs = gst[gidx]
                opb, ok = bigps()

                def ofn(e, opb=opb, heads=heads, slot=slot, cs=cs, c=c):
                    inst = None
                    for j, h in enumerate(heads):
                        o = opb[:, j * 128:(j + 1) * 128]
                        e.matmul(o, qt[:, h, cs], Sbf[:, slot[j], :], start=True, stop=False)
                        inst = e.matmul(o, scT[:, slot[j], :], Vt[:, c, h * 128:(h + 1) * 128], start=False, stop=True)
                    return inst
                P.add("tensor", ofn, [("qt", h) for h in heads] + [("Sbf", r) for r in slot]
                      + [("scT", r, i) for r in slot for i in range(2)] + [("Vt", c)], [ok])
                q = (gidx % 2) * 3
                ss = sm[:, SM_R + q:SM_R + q + 3]
                ls = sm[:, SM_R + 6 + q:SM_R + 6 + q + 3]
                rs = sm[:, SM_R + 12 + q:SM_R + 12 + q + 3]
                for j in range(3):
                    act(junk[:, q + j, :], opb[:, j * 128:(j + 1) * 128], AF.Square, [ok], [("ss", q + j), ("junk", q + j)], accum_out=ss[:, j:j + 1])
                act(ls, ss, AF.Ln, [("ss", q + j) for j in range(3)], [("ls", q)], scale=1.0 / 128, bias=EPS)
                act(rs, ls, AF.Exp, [("ls", q)], [("rs", q)], scale=-0.5)
                return opb, ok, rs, q

            def seg_L(gidx, opb, ok, rs, q):
                heads, slot, c, cs = gst[gidx]
                s0 = slot[0]
                h0 = heads[0]
                dve(lambda e, opb=opb, s0=s0, rs=rs: e.tensor_tensor(
                    out=onb[:, s0:s0 + 3, :], in0=opb[:, 0:384].rearrange("p (j v) -> p j v", v=128),
                    in1=rs[:, 0:3].unsqueeze(2).to_broadcast([128, 3, 128]), op=ALU.mult),
                    [ok, ("rs", q)], [("onb", r) for r in slot])
                t2b, t2k = bfps()

                def t2fn(e, t2b=t2b, slot=slot):
                    inst = None
                    for j in range(3):
                        inst = e.transpose(t2b[:, j * 128:(j + 1) * 128], onb[:, slot[j], :], identb[:])
                    return inst
                P.add("tensor", t2fn, [("onb", r) for r in slot] + ["identb"], [t2k])
                dve(lambda e, t2b=t2b, h0=h0, cs=cs: e.tensor_tensor(
                    out=headsT[:, h0:h0 + 3, cs], in0=t2b[:, 0:384].rearrange("p (j t) -> p j t", t=128),
                    in1=sgb[:, h0:h0 + 3, cs], op=ALU.mult),
                    [t2k] + [("sgb", h) for h in heads], [("headsT", h) for h in heads])

            nxt = seg_E(0)
            for gidx in range(len(groups)):
                cur = nxt
                m = seg_M(gidx)
                seg_U(gidx, *cur)
                if gidx + 1 < len(groups):
                    nxt = seg_E(gidx + 1)
                seg_L(gidx, *m)
                if inter and gidx % 2 == 1:
                    inter.pop(0)()
            while inter:
                inter.pop(0)()

        def gmlp_layer(inter=()):
            inter = list(inter)
            rmsnorm_fm("mixn1")
            w, wkey = get_weight("l1_v")
            w3 = w[:, 0:KD * 768].rearrange("p (k c) -> p k c", c=768)
            for c in range(NC):
                for half in range(2):
                    ps, pk = bigps()
                    mm_group(ps[:, 0:384], [(hT[:, kc, c * 128:(c + 1) * 128], w3[:, kc, half * 384:(half + 1) * 384])
                                            for kc in range(KD)], [wkey] + hT_keys, [pk])
                    act(vg[:, c, half * 384:(half + 1) * 384], ps[:, 0:384], AF.Gelu, [pk], vgk(c, half))
                    dve(lambda e, c=c, half=half: e.bn_stats(out=st6[:, c, half * 6:(half + 1) * 6],
                                                             in_=vg[:, c, half * 384:(half + 1) * 384]),
                        vgk(c, half), [("st6", c, half)])
                dve(lambda e, c=c: e.bn_aggr(out=mv[:, c, :], in_=st6[:, c, :]),
                    [("st6", c, 0), ("st6", c, 1)], [("mv", c)])
            for a in range(2):
                w, wkey = get_weight("l1_u%d" % a)
                w3 = w[:, 0:KD * 512].rearrange("p (k c) -> p k c", c=512)
                for bq in range(4):
                    ps, pk = bigps()
                    mm_group(ps[:], [(w3[:, kc, bq * 128:(bq + 1) * 128], hT[:, kc, :]) for kc in range(KD)],
                             [wkey] + hT_keys, [pk])
                    blk = a * 4 + bq
                    if blk < 6:
                        act(uT[:, blk, :], ps[:], AF.Gelu, [pk], [("uT", blk), "uT", UTA[blk]])
                    else:
                        act(qmT[:, blk - 6, :], ps[:], AF.Copy, [pk], [("qmT", blk - 6)])
            lv = sm[:, SM_L:SM_L + NC]
            rv = sm[:, SM_L + NC:SM_L + 2 * NC]
            act(lv, mv[:, :, 1], AF.Ln, [("mv", c) for c in range(NC)], ["lv"], bias=EPS)
            act(rv, lv, AF.Exp, ["lv"], ["rv"], scale=-0.5)
            for c in range(NC):
                cs = slice(c * 128, (c + 1) * 128)
                dve(lambda e, c=c: e.tensor_scalar(out=Vt[:, c, :], in0=vg[:, c, :], scalar1=mv[:, c, 0:1],
                                                   scalar2=rv[:, c:c + 1], op0=ALU.subtract, op1=ALU.mult),
                    vgk(c, 0) + vgk(c, 1) + [("mv", c), "rv"], [("Vt", c)])
                for gh in range(2):
                    spb, sk = bigps()

                    def sfn(e, spb=spb, gh=gh, c=c):
                        inst = None
                        for gi in range(3):
                            g = gh * 3 + gi
                            inst = e.matmul(spb[:, gi * 128:(gi + 1) * 128], Vt[:, c, g * 128:(g + 1) * 128], wsb[:, g, :],
                                            start=True, stop=True)
                        return inst
                    P.add("tensor", sfn, [("Vt", c)] + [("wsb", g) for g in range(NH)], [sk])
                    r0 = rot("gmb", 2) * 3
                    for gi in range(3):
                        g = gh * 3 + gi
                        sps = spb[:, gi * 128:(gi + 1) * 128]
                        dve(lambda e, sps=sps, g=g, r=r0 + gi: e.scalar_tensor_tensor(
                            out=svt[:, r, :], in0=sps, scalar=pv[:, PV["lng"] + g:PV["lng"] + g + 1], in1=gbias[:, g, :],
                            op0=ALU.mult, op1=ALU.add), [sk, "pv", ("gbias", g)], [("svt", r0 + gi)])
                    g0 = gh * 3
                    dve(lambda e, g0=g0, r0=r0, cs=cs: e.tensor_tensor(out=headsT[:, g0:g0 + 3, cs], in0=svt[:, r0:r0 + 3, :],
                                                                     in1=uT[:, g0:g0 + 3, cs], op=ALU.mult),
                        [("svt", r0 + i) for i in range(3)] + [("uT", g0 + i) for i in range(3)] + [UTA[g0 + i] for i in range(3)],
                        [("headsT", g0 + i) for i in range(3)])
                if inter:
                    inter.pop(0)()
            while inter:
                inter.pop(0)()

        SQ2SL = [6, 7, 10, 11]

        def sq2(kc):
            return hgA[:, SQ2SL[kc // 2], :].bitcast(BF16)[:, (kc % 2) * T:(kc % 2 + 1) * T]

        def sq2k(kc):
            sl = kc // 2
            return [("E1", sl, c) for c in range(NC)] if sl < 2 else [("qsb", sl - 2)]

        def pre_rmsnorm_next(nsel):
            Xn = xbufs[nsel]
            for kc in range(KD):
                act(sq2(kc), Xn[:, kc, :], AF.Square, [("xT%d" % nsel, kc)], sq2k(kc))
            ps, pk = bigps()
            mm_group(ps[:], [(onesb[:], sq2(kc)) for kc in range(KD)],
                     ["onesb"] + [k for kc in range(0, KD, 2) for k in sq2k(kc)], [pk])
            act(lnv[:], ps[:], AF.Ln, [pk], ["lnv"], scale=1.0 / D, bias=EPS)
            act(rstd2[:], lnv[:], AF.Exp, ["lnv"], ["rstd2"], scale=-0.5)
            for kc in range(KD):
                dve(lambda e, kc=kc, Xn=Xn: e.scalar_tensor_tensor(out=hT[:, kc, :], in0=Xn[:, kc, :],
                                                                   scalar=pv[:, PV["mixn0"] + kc:PV["mixn0"] + kc + 1],
                                                                   in1=rstd2[:], op0=ALU.mult, op1=ALU.mult),
                    [("xT%d" % nsel, kc), "pv", "rstd2"], [("hT", kc)])

        def load_x(tt, sel):
            for kc in range(KD):
                P.add("sync", lambda e, tt=tt, kc=kc, sel=sel: e.dma_start(out=xbufs[sel][:, kc, :],
                                                                         in_=xT_v[:, kc, tt * T:(tt + 1) * T]),
                      writes=[("xT%d" % sel, kc)], dma=True, semkey="xT%d_%d" % (sel, kc))

        FSL = [0, 1, 2, 3, 4, 5, 8, 9]
        FKEY = UTA + [("gs", 0), ("gs", 1)]

        def final_norm(sel, t0):
            X = xbufs[sel]
            if final:
                for kc in range(KD):
                    act(sq[:, kc, :], X[:, kc, :], AF.Square, [("xT%d" % sel, kc)], [SQK[kc]])
                ps, pk = bigps()
                mm_group(ps[:], [(onesb[:], sq[:, kc, :]) for kc in range(KD)], ["onesb"] + SQK, [pk])
                act(lnv[:], ps[:], AF.Ln, [pk], ["lnv"], scale=1.0 / D, bias=EPS)
                act(rstd[:], lnv[:], AF.Exp, ["lnv"], ["rstd"], scale=-0.5)
                for kc in range(KD):
                    dve(lambda e, kc=kc, X=X: e.scalar_tensor_tensor(
                        out=hgA[:, FSL[kc], :], in0=X[:, kc, :], scalar=pv[:, PV["finn"] + kc:PV["finn"] + kc + 1],
                        in1=rstd[:], op0=ALU.mult, op1=ALU.mult), [("xT%d" % sel, kc), "pv", "rstd"], [FKEY[kc]])
                    P.add("sync", lambda e, kc=kc, t0=t0: e.dma_start(out=outT_v[:, kc, t0:t0 + T], in_=hgA[:, FSL[kc], :]),
                          reads=[FKEY[kc]], dma=True, semkey="out%d" % kc)
            else:
                for kc in range(KD):
                    P.add("sync", lambda e, kc=kc, t0=t0, X=X: e.dma_start(out=outT_v[:, kc, t0:t0 + T], in_=X[:, kc, :]),
                          reads=[("xT%d" % sel, kc)], dma=True, semkey="outx%d" % kc)

        pending = []
        for ti in range(ntiles):
            t0 = ti * T
            sel = ti % 2
            xsel[0] = sel
            if ti == 0:
                load_x(0, 0)

            def after_iv(ti=ti):
                while pending:
                    pending.pop(0)()
                if ti + 1 < ntiles:
                    load_x(ti + 1, (ti + 1) % 2)

            pre = False
            if 0 in layers:
                hgrn_layer(mem_attn_parts(0), pre_done=pre, after_iv=after_iv)
                out_proj(0)
                ffn(0)
            else:
                after_iv()
                for n in tile_names:
                    if n.startswith("l0"):
                        get_weight(n)
            if 1 in layers:
                gmlp_layer(mem_attn_parts(1))
                out_proj(1)
                nxt_pre = False
                ffn(1, mid=(lambda ti=ti: pre_rmsnorm_next((ti + 1) % 2)) if nxt_pre else None)
            else:
                for n in tile_names:
                    if n.startswith("l1"):
                        get_weight(n)
            pending.append(lambda sel=sel, t0=t0: final_norm(sel, t0))
        while pending:
            pending.pop(0)()
        P.emit(nc, st)
    return nc


_CACHE = {}


def kernel(**inputs):
    shared, per_core, offs, setup_names, tile_names = _pack_host(inputs)
    ftot = shared["wall"].shape[1]
    nc = build_program(offs, setup_names, tile_names, ftot)
    in_maps = []
    for b in range(8):
        m = dict(shared)
        m.update(per_core[b])
        in_maps.append(m)
    res = run_bass_kernel_spmd(nc, in_maps, core_ids=list(range(8)))
    out = np.stack([np.ascontiguousarray(res.results[b]["outT"].T) for b in range(8)], axis=0)
    return out.astype(np.float32)
```

```python
import os
import numpy as np
from contextlib import ExitStack
import concourse.bass as bass
import concourse.mybir as mybir
from concourse.bass_utils import run_bass_kernel_spmd

F32 = mybir.dt.float32
BF16 = mybir.dt.bfloat16
AF = mybir.ActivationFunctionType
ALU = mybir.AluOpType
AX = mybir.AxisListType

D = 1024
S = 4096
KD = 8
DTOK = 768
NH = 6
DFF = 2816
NFF = 22
MEM = 256
EPS = 1e-6
T = 512
NC = T // 128
NSLOT = 4
SLOTW = 6144
EPOCH = 3000
DBG_STOP = int(os.environ.get('DBG_STOP', '99'))
DBG_SUB = int(os.environ.get('DBG_SUB', '99'))


class _Op:
    __slots__ = ("eng", "fn", "deps", "dma", "semkey", "inc", "semref", "semval")


class Prog:
    def __init__(self):
        self.ops = []
        self.last_w = {}
        self.readers = {}

    def add(self, eng, fn, reads=(), writes=(), dma=False, semkey=None):
        i = len(self.ops)
        deps = set()
        for k in reads:
            w = self.last_w.get(k)
            if w is not None:
                deps.add(w)
        for k in writes:
            w = self.last_w.get(k)
            if w is not None:
                deps.add(w)
            for r in self.readers.get(k, ()):
                deps.add(r)
        for k in reads:
            self.readers.setdefault(k, []).append(i)
        for k in writes:
            self.last_w[k] = i
            self.readers[k] = []
        deps.discard(i)
        op = _Op()
        op.eng = eng
        op.fn = fn
        op.deps = sorted(deps)
        op.dma = dma
        op.semkey = semkey
        op.inc = False
        op.semref = None
        op.semval = 0
        self.ops.append(op)
        return i

    def emit(self, nc, stack, final_wait_engine="sync"):
        ops = self.ops
        for op in ops:
            for d in op.deps:
                dop = ops[d]
                if dop.eng == "tensor" and op.eng == "tensor" and not dop.dma:
                    continue
                dop.inc = True
        cnt = {}
        dcnt = {}
        semnames = []
        finals = {}
        for op in ops:
            if op.dma:
                name = "d_" + str(op.semkey)
                dcnt[name] = dcnt.get(name, 0) + 16
                op.semref = name
                op.semval = dcnt[name]
                finals[name] = op.semval
            elif op.inc:
                c = cnt.get(op.eng, 0)
                name = "e_%s_%d" % (op.eng, c // EPOCH)
                op.semref = name
                op.semval = (c % EPOCH) + 1
                cnt[op.eng] = c + 1
            else:
                continue
            if name not in semnames:
                semnames.append(name)
        sems = {}
        for name in semnames:
            sems[name] = stack.enter_context(nc.semaphore(name))
        block = stack.enter_context(nc.Block())

        def make(engname):
            def body(eng):
                waited = {}
                for op in ops:
                    if op.eng != engname:
                        continue
                    need = {}
                    for d in op.deps:
                        dop = ops[d]
                        if dop.eng == "tensor" and engname == "tensor" and not dop.dma:
                            continue
                        if dop.semval > need.get(dop.semref, 0):
                            need[dop.semref] = dop.semval
                    for ref, val in need.items():
                        if waited.get(ref, 0) >= val:
                            continue
                        eng.wait_ge(sems[ref], val)
                        waited[ref] = val
                    inst = op.fn(eng)
                    if op.dma:
                        inst.then_inc(sems[op.semref], 16)
                    elif op.inc:
                        inst.then_inc(sems[op.semref], 1)
                if engname == final_wait_engine:
                    for name, val in finals.items():
                        if waited.get(name, 0) < val:
                            eng.wait_ge(sems[name], val)
            return body

        used = set(op.eng for op in ops) | {final_wait_engine}
        for engname in ["sync", "gpsimd", "tensor", "scalar", "vector"]:
            if engname in used:
                getattr(block, engname)(make(engname))


def _lin_block(W, cols):
    kc = W.shape[0] // 128
    sub = W[:, cols]
    return np.ascontiguousarray(sub.reshape(kc, 128, sub.shape[1]).transpose(1, 0, 2))


def _r(a, b):
    return np.arange(a, b)


def _weight_blocks(inp):
    setup = []
    for l in range(2):
        setup.append(("kv%d" % l, _lin_block(inp["w_mem_kv"][l], _r(0, 512)).reshape(128, -1)))
    tile = []
    W = inp["hg_w_in"][0]
    tile.append(("l0_iv", _lin_block(W, _r(1536, 2304)).reshape(128, -1)))
    for h in range(NH):
        blk = np.stack([_lin_block(W, _r(h * 128, h * 128 + 128)),
                        _lin_block(W, _r(768 + h * 128, 768 + h * 128 + 128)),
                        _lin_block(W, _r(2304 + h * 128, 2304 + h * 128 + 128))], axis=1)
        tile.append(("l0_h%d" % h, blk.reshape(128, -1)))
    tile.append(("l0_qm", _lin_block(W, _r(3072, 3328)).reshape(128, -1)))
    G = inp["gm_w_in"][0]
    for l in range(2):
        if l == 1:
            tile.append(("l1_v", _lin_block(G, _r(768, 1536)).reshape(128, -1)))
            tile.append(("l1_u0", _lin_block(G, _r(0, 512)).reshape(128, -1)))
            tile.append(("l1_u1", _lin_block(G, np.concatenate([_r(512, 768), _r(1536, 1792)])).reshape(128, -1)))
        for a in range(2):
            tile.append(("l%d_wo%d" % (l, a), _lin_block(inp["w_out"][l], _r(a * 512, a * 512 + 512)).reshape(128, -1)))
        FI = inp["w_ffn_in"][l]
        for c2 in range(NFF // 2):
            parts = []
            for cc in range(2):
                c = 2 * c2 + cc
                parts.append(np.stack([_lin_block(FI, _r(c * 128, c * 128 + 128)),
                                       _lin_block(FI, _r(DFF + c * 128, DFF + c * 128 + 128))], axis=1))
            blk = np.stack(parts, axis=1)
            tile.append(("l%d_fi%d" % (l, c2), blk.reshape(128, -1)))
        FO = inp["w_ffn_out"][l]
        for j2 in range(4):
            tile.append(("l%d_fo%d" % (l, j2), _lin_block(FO, _r(j2 * 256, j2 * 256 + 256)).reshape(128, -1)))
    return setup, tile


PV = {}
_c = 0
for _n, _w in [("mixn0", 8), ("mixn1", 8), ("ffnn0", 8), ("ffnn1", 8), ("finn", 8), ("memn0", 8), ("memn1", 8),
               ("hglb", 18), ("onorm", 6), ("lng", 6)]:
    PV[_n] = _c
    _c += _w
NPV = _c


def _pack_host(inp):
    inp = {k: np.asarray(v, dtype=np.float32) for k, v in inp.items()}
    setup, tile = _weight_blocks(inp)
    offs = {}
    o = 0
    for n, a in setup + tile:
        offs[n] = (o, a.shape[1])
        o += a.shape[1]
    wall = np.concatenate([a for _, a in setup + tile], axis=1)
    pv = np.zeros((128, NPV), np.float32)

    def put(name, vec):
        k = vec.shape[0] // 128
        pv[:, PV[name]:PV[name] + k] = vec.reshape(k, 128).T

    for l in range(2):
        put("mixn%d" % l, inp["mix_norm"][l])
        put("ffnn%d" % l, inp["ffn_norm"][l])
        put("memn%d" % l, inp["mem_norm"][l])
    put("finn", inp["final_norm"])
    put("hglb", inp["hg_lb"].reshape(-1))
    put("onorm", inp["hg_onorm"][0])
    put("lng", inp["gm_ln_g"][0])
    cst = np.zeros((128, 256), np.float32)
    cst[:, 0:128] = np.eye(128, dtype=np.float32)
    cst[:, 128:256] = np.triu(np.ones((128, 128), np.float32))
    gmc = np.zeros((128, 3 * 768), np.float32)
    gmc[:, 0:768] = inp["gm_ws"][0].transpose(2, 0, 1).reshape(128, 768)
    gmc[:, 768:1536] = np.broadcast_to(inp["gm_ln_b"][0][None, :], (128, 768))
    gmc[:, 1536:2304] = np.broadcast_to(inp["gm_bs"][0].reshape(1, 768), (128, 768))
    shared = {"wall": wall, "pvec": pv, "cst": cst, "gmc": gmc}
    per_core = []
    for b in range(8):
        per_core.append({"xT": np.ascontiguousarray(inp["x"][b].T),
                         "memT": np.ascontiguousarray(inp["mem"][b].T)})
    return shared, per_core, offs, [n for n, _ in setup], [n for n, _ in tile]


def build_program(offs, setup_names, tile_names, ftot, ntiles=S // T, layers=(0, 1), final=True, dbg=None):
    nc = bass.Bass("TRN2", target_bir_lowering=False)
    xT_d = nc.dram_tensor("xT", [D, S], F32, kind="ExternalInput").ap()
    memT_d = nc.dram_tensor("memT", [D, MEM], F32, kind="ExternalInput").ap()
    wall_d = nc.dram_tensor("wall", [128, ftot], F32, kind="ExternalInput").ap()
    pvec_d = nc.dram_tensor("pvec", [128, NPV], F32, kind="ExternalInput").ap()
    cst_d = nc.dram_tensor("cst", [128, 256], F32, kind="ExternalInput").ap()
    gmc_d = nc.dram_tensor("gmc", [128, 2304], F32, kind="ExternalInput").ap()
    outT_d = nc.dram_tensor("outT", [D, S], F32, kind="ExternalOutput").ap()
    xT_v = xT_d.rearrange("(kc p) s -> p kc s", p=128)
    outT_v = outT_d.rearrange("(kc p) s -> p kc s", p=128)
    dbg = dbg or {}
    dbg_d = {}
    for name, shape in dbg.items():
        dbg_d[name] = nc.dram_tensor("dbg_" + name, list(shape), F32, kind="ExternalOutput").ap()

    P = Prog()
    with ExitStack() as st:
        def sb(name, shape, dt):
            return st.enter_context(nc.sbuf_tensor("s_" + name, shape, dt))

        def pst(name, shape, dt):
            return st.enter_context(nc.psum_tensor(name, shape, dt))

        xbufs = [sb("xTa", [128, KD, T], F32), sb("xTb", [128, KD, T], F32)]
        xT = xbufs[0]
        hT = sb("hT", [128, KD, T], BF16)
        lnv = sb("lnv", [128, T], F32)
        rstd = sb("rstd", [128, T], F32)
        rstd2 = rstd
        Vt = sb("Vt", [128, NC, DTOK], BF16)
        hgA = sb("hgA", [128, 12, T], F32)
        L1 = hgA[:, 0:2, :]
        La = hgA[:, 2:4, :]
        bb = hgA[:, 4:6, :]
        E1 = hgA[:, 6:8, :]
        gs = hgA[:, 8:10, :]
        qsb = hgA[:, 10:12, :]
        uT = hgA[:, 0:NH, :]
        UTA = [("L1", 0), ("L1", 1), ("La", 0), ("La", 1), ("bb", 0), ("bb", 1)]
        qt = sb("qt", [128, NH, T], BF16)
        kt = sb("kt", [128, NH, T], BF16)
        sgb = sb("sgb", [128, NH, T], BF16)
        ktok = sb("ktok", [128, NH, T], BF16)
        scT = sb("scT", [128, 6, 128], BF16)
        Sbf = sb("Sbf", [128, 6, 128], BF16)
        onb = sb("onb", [128, 6, 128], BF16)
        svt = sb("svt", [128, 6, 128], F32)
        junk = sb("junk", [128, 6, 128], BF16)
        headsT = sb("headsT", [128, KD, T], BF16)
        qmT = sb("qmT", [128, 2, T], BF16)
        actT = sb("actT", [128, NFF, T], BF16)
        sgt = sb("sgt", [128, 2, T], F32)
        lnd = sgt[:, 0, :]
        rd = sgt[:, 1, :]
        eT = actT[:, 0:4, :]
        sq = actT[:, 0:KD, :]
        SQK = [("actT", c) for c in range(KD)]
        vg = actT[:, 8:20, :].rearrange("p a b -> p (a b)").bitcast(F32).rearrange("p (c f) -> p c f", f=DTOK)

        def vgk(c, half):
            st_ = c * 3072 + half * 1536
            return [("vg", c, half)] + [("actT", 8 + i) for i in range(st_ // 1024, (st_ + 1535) // 1024 + 1)]
        ring = sb("ring", [128, NSLOT, SLOTW], BF16)
        U = sb("U", [128, NH, 128], F32)
        KT = sb("KT", [128, 2, 2, MEM], BF16)
        Vm = sb("Vm", [128, 2, 2, 256], BF16)
        wsb = sb("wsb", [128, NH, 128], BF16)
        wsm = sgt.rearrange("p a b -> p (a b)")[:, 0:768].rearrange("p (g t) -> p g t", t=128)
        gbias = sb("gbias", [128, NH, 128], F32)
        pv = sb("pv", [128, NPV], F32)
        cst = sb("cst", [128, 256], F32)
        identb = sb("identb", [128, 128], BF16)
        trii = sb("trii", [128, 128], mybir.dt.int32)
        trif = cst[:, 128:256]
        onesb = sb("onesb", [128, 128], BF16)
        onesf = sb("onesf", [128, 128], F32)
        sm = sb("sm", [128, 30 + NH * 6 * NC + 18 + 2 * NC + 6 + 2], F32)
        LB, OML, NOML, BLAST, BMIDL = 0, 6, 12, 18, 24
        SM_T = 30
        SM_H = 30
        SM_R = SM_H + NH * 6 * NC
        SM_L = SM_R + 18
        LNOML = SM_L + 2 * NC
        st6 = sb("st6", [128, NC, 12], F32)
        mv = sb("mv", [128, NC, 2], F32)

        psA = [pst("psA%d" % i, [128, 512], F32) for i in range(6)]
        psB = [pst("psB%d" % i, [128, 1024], BF16) for i in range(2)]
        ctr = {"A": 0, "B": 0, "rot": {}}

        def bigps():
            i = ctr["A"] % 6
            ctr["A"] += 1
            return psA[i], ("psA", i)

        def bfps():
            i = ctr["B"] % 2
            ctr["B"] += 1
            return psB[i], ("psB", i)

        def rot(name, n=2):
            i = ctr["rot"].get(name, 0)
            ctr["rot"][name] = i + 1
            return i % n

        def mm_group(out_ap, pairs, reads, writes):
            pairs = list(pairs)

            def fn(e):
                inst = None
                n = len(pairs)
                for i, (l, r) in enumerate(pairs):
                    inst = e.matmul(out_ap, l, r, start=(i == 0), stop=(i == n - 1))
                return inst
            P.add("tensor", fn, reads, writes)

        def act(out, in_, func, reads, writes, **kw):
            P.add("scalar", lambda e: e.activation(out=out, in_=in_, func=func, **kw), reads, writes)

        def dve(fn, reads, writes):
            P.add("vector", fn, reads, writes)

        def dump(name, ap, keys):
            if name in dbg_d:
                P.add("gpsimd", lambda e: e.dma_start(out=dbg_d[name], in_=ap), reads=keys, dma=True, semkey="dbg_" + name)

        seq = list(setup_names) + [n for _ in range(ntiles) for n in tile_names]
        wst = {"issued": 0, "next": 0}

        def issue_weight(i):
            name = seq[i]
            off, w = offs[name]
            slot = i % NSLOT
            b = 1024 if w % 1024 == 0 else 512
            src = wall_d[:, off:off + w].rearrange("p (a b) -> p a b", b=b)
            dst = ring[:, slot, 0:w].rearrange("p (a b) -> p a b", b=b)
            P.add("gpsimd", lambda e: e.dma_start(out=dst, in_=src), writes=[("ring", slot)], dma=True,
                  semkey="ring%d" % slot)

        def get_weight(name):
            i = wst["next"]
            while seq[i] != name:
                i += 1
            wst["next"] = i + 1
            while wst["issued"] < min(len(seq), i + NSLOT):
                issue_weight(wst["issued"])
                wst["issued"] += 1
            slot = i % NSLOT
            return ring[:, slot, :], ("ring", slot)

        P.add("sync", lambda e: e.dma_start(out=pv[:], in_=pvec_d), writes=["pv"], dma=True, semkey="pv")
        P.add("sync", lambda e: e.dma_start(out=cst[:], in_=cst_d), writes=["cst"], dma=True, semkey="cst")
        uTf = uT.rearrange("p a b -> p (a b)")
        P.add("sync", lambda e: e.dma_start(out=uTf[:, 0:2304], in_=gmc_d), writes=["uT"] + UTA, dma=True, semkey="gmc")
        dve(lambda e: e.tensor_copy(out=identb[:], in_=cst[:, 0:128]), ["cst"], ["identb"])
        dve(lambda e: e.memset(onesb[:], 1.0), [], ["onesb"])
        dve(lambda e: e.memset(onesf[:], 1.0), [], ["onesf"])
        dve(lambda e: e.memset(U[:], 0.0), [], [("U", h) for h in range(NH)])
        dve(lambda e: e.memset(sm[:], 0.0), [], ["sm"] + [("blast", h) for h in range(NH)] + [("bmidl", h) for h in range(NH)])
        dve(lambda e: e.memset(scT[:], 0.0), [], [("scT", r, i) for r in range(6) for i in range(2)])
        dve(lambda e: e.tensor_copy(out=trii[:], in_=cst[:, 128:256]), ["cst"], ["trii"])
        lbx = sm[:, SM_T:SM_T + 18]
        act(lbx, pv[:, PV["hglb"]:PV["hglb"] + 18], AF.Exp, ["pv", "sm"], ["lbx"])
        den = sm[:, SM_T + 18:SM_T + 24]
        dve(lambda e: e.tensor_tensor(out=den, in0=lbx[:, 0:6], in1=lbx[:, 6:12], op=ALU.add), ["lbx"], ["den"])
        dve(lambda e: e.tensor_tensor(out=den, in0=den, in1=lbx[:, 12:18], op=ALU.add), ["lbx", "den"], ["den"])
        lnden = sm[:, SM_T + 24:SM_T + 30]
        act(lnden, den, AF.Ln, ["den"], ["lnden"])
        dve(lambda e: e.tensor_tensor(out=lnden, in0=pv[:, PV["hglb"]:PV["hglb"] + 6], in1=lnden, op=ALU.subtract),
            ["pv", "lnden"], ["lnden"])
        act(sm[:, LB:LB + 6], lnden, AF.Exp, ["lnden"], ["lb"])
        dve(lambda e: e.tensor_scalar(out=sm[:, OML:OML + 6], in0=sm[:, LB:LB + 6], scalar1=-1.0, scalar2=1.0,
                                      op0=ALU.mult, op1=ALU.add), ["lb"], ["oml"])
        act(sm[:, LNOML:LNOML + 6], sm[:, OML:OML + 6], AF.Ln, ["oml"], ["lnoml"])
        dve(lambda e: e.tensor_scalar(out=sm[:, NOML:NOML + 6], in0=sm[:, LB:LB + 6], scalar1=-1.0, scalar2=None,
                                      op0=ALU.add), ["lb"], ["noml"])
        for g in range(NH):
            dve(lambda e, g=g: e.tensor_tensor(out=wsm[:, g, :], in0=uTf[:, g * 128:(g + 1) * 128], in1=trif,
                                               op=ALU.mult), ["uT", "cst"], [("wsm", g), ("sgt", g // 4)])
            dve(lambda e, g=g: e.tensor_copy(out=wsb[:, g, :], in_=wsm[:, g, :]), [("wsm", g), ("sgt", g // 4)], [("wsb", g)])
            psb_, pk = bigps()
            ps = psb_[:, 0:128]

            def gb(e, g=g, ps=ps):
                e.matmul(ps, uTf[:, 768 + g * 128:768 + (g + 1) * 128], wsm[:, g, :], start=True, stop=False)
                return e.matmul(ps, onesf[0:1, 0:128], uTf[0:1, 1536 + g * 128:1536 + (g + 1) * 128],
                                start=False, stop=True)
            P.add("tensor", gb, ["uT", ("wsm", g), ("sgt", g // 4), "onesf"], [pk])
            dve(lambda e, g=g, ps=ps: e.tensor_copy(out=gbias[:, g, :], in_=ps), [pk], [("gbias", g)])

        hT_keys = [("hT", kc) for kc in range(KD)]

        xsel = [0]

        def XT():
            return xbufs[xsel[0]]

        def XK(j):
            return ("xT%d" % xsel[0], j)

        def rmsnorm_stats(n=T):
            X = XT()
            for kc in range(KD):
                act(sq[:, kc, 0:n], X[:, kc, 0:n], AF.Square, [XK(kc)], [SQK[kc]])
            ps, pk = bigps()
            mm_group(ps[:, 0:n], [(onesb[:], sq[:, kc, 0:n]) for kc in range(KD)], ["onesb"] + SQK, [pk])
            act(lnv[:, 0:n], ps[:, 0:n], AF.Ln, [pk], ["lnv"], scale=1.0 / D, bias=EPS)
            act(rstd[:, 0:n], lnv[:, 0:n], AF.Exp, ["lnv"], ["rstd"], scale=-0.5)

        def rmsnorm_fm(gname):
            rmsnorm_stats()
            X = XT()
            for kc in range(KD):
                dve(lambda e, kc=kc, X=X: e.scalar_tensor_tensor(out=hT[:, kc, :], in0=X[:, kc, :],
                                                                 scalar=pv[:, PV[gname] + kc:PV[gname] + kc + 1],
                                                                 in1=rstd[:], op0=ALU.mult, op1=ALU.mult),
                    [XK(kc), "pv", "rstd"], [("hT", kc)])

        for l in range(2):
            if l not in layers:
                get_weight("kv%d" % l)
                continue
            P.add("sync", lambda e: e.dma_start(out=xT[:, :, 0:MEM], in_=memT_d.rearrange("(kc p) m -> p kc m", p=128)),
                  writes=[XK(j) for j in range(KD)], dma=True, semkey="xT")
            rmsnorm_stats(MEM)
            gname = "memn%d" % l
            for kc in range(KD):
                dve(lambda e, kc=kc, gname=gname: e.scalar_tensor_tensor(
                    out=hT[:, kc, 0:MEM], in0=xT[:, kc, 0:MEM], scalar=pv[:, PV[gname] + kc:PV[gname] + kc + 1],
                    in1=rstd[:, 0:MEM], op0=ALU.mult, op1=ALU.mult), [XK(kc), "pv", "rstd"], [("hT", kc)])
            wk, wkey = get_weight("kv%d" % l)
            wk3 = wk[:, 0:KD * 512].rearrange("p (k c) -> p k c", c=512)
            for blk in range(2):
                ps, pk = bigps()
                mm_group(ps[:, 0:MEM], [(wk3[:, kc, blk * 128:(blk + 1) * 128], hT[:, kc, 0:MEM]) for kc in range(KD)],
                         [wkey] + hT_keys, [pk])
                dve(lambda e, ps=ps, blk=blk, l=l: e.tensor_copy(out=KT[:, l, blk, :], in_=ps[:, 0:MEM]),
                    [pk], [("KT", l)])
            for mc in range(2):
                ps, pk = bigps()
                mm_group(ps[:, 0:256], [(hT[:, kc, mc * 128:(mc + 1) * 128], wk3[:, kc, 256:512]) for kc in range(KD)],
                         [wkey] + hT_keys, [pk])
                dve(lambda e, ps=ps, mc=mc, l=l: e.tensor_copy(out=Vm[:, l, mc, :], in_=ps[:, 0:256]),
                    [pk], [("Vm", l)])

        def mem_attn_parts(l):
            def scores(pr):
                for hh in range(2):
                    base = hh * 64
                    for mc in range(2):
                        ps, pk = bigps()
                        mm_group(ps[:], [(KT[base:base + 64, l, pr, mc * 128:(mc + 1) * 128], qmT[base:base + 64, pr, :])],
                                 [("KT", l), ("qmT", pr)], [pk])
                        act(eT[:, hh * 2 + mc, :], ps[:], AF.Exp, [pk], [("actT", hh * 2 + mc)], scale=0.125)

            def pv_(pr):
                nps, nk = bigps()
                dps, dk = bigps()
                for hh in range(2):
                    h = pr * 2 + hh
                    mm_group(nps[hh * 64:(hh + 1) * 64, :],
                             [(Vm[:, l, mc, h * 64:(h + 1) * 64], eT[:, hh * 2 + mc, :]) for mc in range(2)],
                             [("Vm", l)] + [("actT", hh * 2 + mc) for mc in range(2)], [nk])
                    mm_group(dps[hh * 64:(hh + 1) * 64, :],
                             [(onesb[:, 0:64], eT[:, hh * 2 + mc, :]) for mc in range(2)],
                             ["onesb"] + [("actT", hh * 2 + mc) for mc in range(2)], [dk])
                act(lnd, dps[:], AF.Ln, [dk], [("sgt", 0)])
                act(rd, lnd, AF.Exp, [("sgt", 0)], [("sgt", 1)], scale=-1.0)
                dve(lambda e, nps=nps, pr=pr: e.tensor_tensor(out=headsT[:, 6 + pr, :], in0=nps[:], in1=rd, op=ALU.mult),
                    [nk, ("sgt", 1)], [("headsT", 6 + pr)])
            return [lambda: scores(0), lambda: pv_(0), lambda: scores(1), lambda: pv_(1)]

        def mem_attn(l):
            for f in mem_attn_parts(l):
                f()

        def out_proj(l):
            for a in range(2):
                w, wkey = get_weight("l%d_wo%d" % (l, a))
                w3 = w[:, 0:KD * 512].rearrange("p (k c) -> p k c", c=512)
                for jj in range(4):
                    j = a * 4 + jj
                    ps, pk = bigps()
                    mm_group(ps[:], [(w3[:, kc, jj * 128:(jj + 1) * 128], headsT[:, kc, :]) for kc in range(KD)],
                             [wkey] + [("headsT", kc) for kc in range(KD)], [pk])
                    dve(lambda e, ps=ps, j=j, X=XT(): e.tensor_tensor(out=X[:, j, :], in0=X[:, j, :], in1=ps[:], op=ALU.add),
                        [pk, XK(j)], [XK(j)])

        def ffn(l, mid=None):
            rmsnorm_fm("ffnn%d" % l)
            for c2 in range(NFF // 2):
                w, wkey = get_weight("l%d_fi%d" % (l, c2))
                w5 = w[:, 0:4096].rearrange("p (a b k c) -> p a b k c", a=2, b=2, k=KD)
                for cc in range(2):
                    c = c2 * 2 + cc
                    gps, gk = bigps()
                    mm_group(gps[:], [(w5[:, cc, 0, kc, :], hT[:, kc, :]) for kc in range(KD)], [wkey] + hT_keys, [gk])
                    ups, uk = bigps()
                    mm_group(ups[:], [(w5[:, cc, 1, kc, :], hT[:, kc, :]) for kc in range(KD)], [wkey] + hT_keys, [uk])
                    r = rot("sgt")
                    act(sgt[:, r, :], gps[:], AF.Silu, [gk], [("sgt", r)])
                    dve(lambda e, r=r, c=c, ups=ups: e.tensor_tensor(out=actT[:, c, :], in0=sgt[:, r, :], in1=ups[:],
                                                                     op=ALU.mult), [("sgt", r), uk], [("actT", c)])
            if mid is not None:
                mid()
            for j2 in range(4):
                w, wkey = get_weight("l%d_fo%d" % (l, j2))
                w3 = w[:, 0:NFF * 256].rearrange("p (k c) -> p k c", c=256)
                for jj in range(2):
                    j = j2 * 2 + jj
                    ps, pk = bigps()
                    mm_group(ps[:], [(w3[:, kc, jj * 128:(jj + 1) * 128], actT[:, kc, :]) for kc in range(NFF)],
                             [wkey] + [("actT", c) for c in range(NFF)], [pk])
                    dve(lambda e, ps=ps, j=j, X=XT(): e.tensor_tensor(out=X[:, j, :], in0=X[:, j, :], in1=ps[:], op=ALU.add),
                        [pk, XK(j)], [XK(j)])

        def hgrn_layer(inter=(), pre_done=False, after_iv=None):
            inter = list(inter)
            if not pre_done:
                rmsnorm_fm("mixn0")
            w, wkey = get_weight("l0_iv")
            w3 = w[:, 0:KD * 768].rearrange("p (k c) -> p k c", c=768)
            for c in range(NC):
                for half in range(2):
                    ps, pk = bigps()
                    mm_group(ps[:, 0:384], [(hT[:, kc, c * 128:(c + 1) * 128], w3[:, kc, half * 384:(half + 1) * 384])
                                            for kc in range(KD)], [wkey] + hT_keys, [pk])
                    act(Vt[:, c, half * 384:(half + 1) * 384], ps[:, 0:384], AF.Copy, [pk], [("Vt", c)])
            if after_iv is not None:
                after_iv()
            qbank = {}

            def front_P(h):
                w, wkey = get_weight("l0_h%d" % h)
                w4 = w[:, 0:3 * KD * 128].rearrange("p (a k c) -> p a k c", a=3, k=KD)
                fps, fk = bigps()
                mm_group(fps[:], [(w4[:, 1, kc, :], hT[:, kc, :]) for kc in range(KD)], [wkey] + hT_keys, [fk])
                gps, gk = bigps()
                mm_group(gps[:], [(w4[:, 2, kc, :], hT[:, kc, :]) for kc in range(KD)], [wkey] + hT_keys, [gk])
                qps, qk = bigps()
                mm_group(qps[:], [(w4[:, 0, kc, :], hT[:, kc, :]) for kc in range(KD)], [wkey] + hT_keys, [qk])
                qbank[h] = (qps, qk, fps, fk, gps, gk)

            def front_A(h):
                p = h % 2
                qps, qk, fps, fk, gps, gk = qbank[h]
                act(L1[:, p, :], fps[:], AF.Exp, [fk], [("L1", p)])
                act(gs[:, p, :], gps[:], AF.Exp, [gk], [("gs", p)], scale=-1.0)
                dve(lambda e, p=p, qps=qps: e.tensor_copy(out=qsb[:, p, :], in_=qps[:]), [qk], [("qsb", p)])
                act(La[:, p, :], L1[:, p, :], AF.Ln, [("L1", p), "lb"], [("La", p)], bias=sm[:, LB + h:LB + h + 1])
                act(L1[:, p, :], L1[:, p, :], AF.Ln, [("L1", p)], [("L1", p)], bias=1.0)
                act(gs[:, p, :], gs[:, p, :], AF.Ln, [("gs", p)], [("gs", p)], bias=1.0)
                act(gs[:, p, :], gs[:, p, :], AF.Exp, [("gs", p)], [("gs", p)], scale=-1.0)
                dve(lambda e, h=h, p=p: e.tensor_tensor_scan(out=bb[:, p, :], data0=La[:, p, :], data1=L1[:, p, :],
                                                             initial=sm[:, BLAST + h:BLAST + h + 1],
                                                             op0=ALU.add, op1=ALU.subtract),
                    [("La", p), ("L1", p), ("blast", h)], [("bb", p)])
                dve(lambda e, h=h, p=p, gps=gps: e.scalar_tensor_tensor(
                    out=sgb[:, h, :], in0=gps[:], scalar=pv[:, PV["onorm"] + h:PV["onorm"] + h + 1], in1=gs[:, p, :],
                    op0=ALU.mult, op1=ALU.mult), [gk, ("gs", p), "pv"], [("sgb", h)])
                bb3 = bb[:, p, :].rearrange("p (c t) -> p c t", t=128)
                bmid = bb3[:, :, 63]
                blc = bb3[:, :, 127]
                hb = SM_H + h * 6 * NC
                nbm = sm[:, hb:hb + NC]
                gin = sm[:, hb + NC:hb + 3 * NC]
                kb = sm[:, hb + 5 * NC:hb + 6 * NC]
                dve(lambda e, nbm=nbm, bmid=bmid: e.tensor_scalar(out=nbm, in0=bmid, scalar1=-1.0, scalar2=None,
                                                                  op0=ALU.mult), [("bb", p)], [("nbm", h)])
                dve(lambda e, kb=kb, bmid=bmid, h=h: e.tensor_scalar(out=kb, in0=bmid, scalar1=sm[:, LNOML + h:LNOML + h + 1],
                                                                     scalar2=None, op0=ALU.add),
                    [("bb", p), "lnoml"], [("kb", h)])
                dve(lambda e, h=h, gin=gin, bmid=bmid: e.tensor_tensor(out=gin[:, 0:1], in0=bmid[:, 0:1],
                                                                       in1=sm[:, BLAST + h:BLAST + h + 1],
                                                                       op=ALU.subtract),
                    [("bb", p), ("blast", h)], [("gin0", h)])
                if NC > 1:
                    dve(lambda e, gin=gin, bmid=bmid, blc=blc: e.tensor_tensor(out=gin[:, 1:NC], in0=bmid[:, 1:NC],
                                                                               in1=blc[:, 0:NC - 1], op=ALU.subtract),
                        [("bb", p)], [("gin1", h)])
                dve(lambda e, gin=gin, bmid=bmid, blc=blc: e.tensor_tensor(out=gin[:, NC:2 * NC], in0=blc, in1=bmid,
                                                                           op=ALU.subtract), [("bb", p)], [("gin2", h)])
                dve(lambda e, h=h, p=p: e.tensor_copy(out=sm[:, BLAST + h:BLAST + h + 1], in_=bb[:, p, T - 1:T]),
                    [("bb", p), ("gin0", h)], [("blast", h)])
                dve(lambda e, p=p: e.tensor_tensor(out=L1[:, p, :], in0=L1[:, p, :], in1=bb[:, p, :], op=ALU.add),
                    [("L1", p), ("bb", p)], [("L1", p)])

            def front_B(h):
                p = h % 2
                bb3 = bb[:, p, :].rearrange("p (c t) -> p c t", t=128)
                bmid = bb3[:, :, 63]
                hb = SM_H + h * 6 * NC
                nbm = sm[:, hb:hb + NC]
                kb = sm[:, hb + 5 * NC:hb + 6 * NC]
                act(sm[:, hb + 3 * NC:hb + 5 * NC], sm[:, hb + NC:hb + 3 * NC], AF.Exp,
                    [("gin0", h), ("gin1", h), ("gin2", h)], [("gam", h)])
                for c in range(NC):
                    cs = slice(c * 128, (c + 1) * 128)
                    act(E1[:, p, cs], bb[:, p, cs], AF.Exp, [("bb", p), ("nbm", h)], [("E1", p, c)], bias=nbm[:, c:c + 1])
                    act(kt[:, h, cs], L1[:, p, cs], AF.Exp, [("L1", p), ("kb", h)], [("kt", h, c)], scale=-1.0,
                        bias=kb[:, c:c + 1])
                dve(lambda e, h=h, p=p: e.tensor_tensor(out=qt[:, h, :], in0=qsb[:, p, :], in1=E1[:, p, :], op=ALU.mult),
                    [("qsb", p)] + [("E1", p, c) for c in range(NC)], [("qt", h)])
                tpsb, tk = bfps()

                def tfn(e, tpsb=tpsb, h=h):
                    inst = None
                    for c in range(NC):
                        inst = e.transpose(tpsb[:, c * 128:(c + 1) * 128], kt[:, h, c * 128:(c + 1) * 128], identb[:])
                    return inst
                P.add("tensor", tfn, [("kt", h, c) for c in range(NC)] + ["identb"], [tk])
                dve(lambda e, tpsb=tpsb, h=h: e.tensor_copy(out=ktok[:, h, :], in_=tpsb[:, 0:T]), [tk], [("ktok", h)])

            front_P(0)
            front_P(1)
            front_A(0)
            for h in range(NH):
                if h + 2 < NH:
                    front_P(h + 2)
                if h + 1 < NH:
                    front_A(h + 1)
                front_B(h)
            w, wkey = get_weight("l0_qm")
            w3 = w[:, 0:KD * 256].rearrange("p (k c) -> p k c", c=256)
            for blk in range(2):
                ps, pk = bigps()
                mm_group(ps[:], [(w3[:, kc, blk * 128:(blk + 1) * 128], hT[:, kc, :]) for kc in range(KD)],
                         [wkey] + hT_keys, [pk])
                act(qmT[:, blk, :], ps[:], AF.Copy, [pk], [("qmT", blk)])

            KTK = lambda h: [("kt", h, c) for c in range(NC)]
            groups = [(c, gi) for c in range(NC) for gi in range(2)]
            gst = {}

            def seg_E(gidx):
                c, gi = groups[gidx]
                c0 = c * 128
                cs = slice(c0, c0 + 128)
                heads = [gi * 3 + j for j in range(3)]
                slot = [(gidx % 2) * 3 + j for j in range(3)]
                spsb, sk = bigps()

                def sfn(e, spsb=spsb, c0=c0, heads=heads):
                    inst = None
                    for j, h in enumerate(heads):
                        o = j * 128
                        e.matmul(spsb[0:64, o:o + 64], kt[:, h, c0:c0 + 64], qt[:, h, c0:c0 + 64], start=True, stop=True)
                        inst = e.matmul(spsb[:, o + 64:o + 128], kt[:, h, c0:c0 + 128], qt[:, h, c0 + 64:c0 + 128],
                                        start=True, stop=True)
                    return inst
                P.add("tensor", sfn, [k for h in heads for k in KTK(h)] + [("qt", h) for h in heads], [sk])
                ppb, pkk = bigps()

                def pfn(e, ppb=ppb, cs=cs, c=c, heads=heads):
                    inst = None
                    for j, h in enumerate(heads):
                        inst = e.matmul(ppb[:, j * 128:(j + 1) * 128], ktok[:, h, cs], Vt[:, c, h * 128:(h + 1) * 128],
                                        start=True, stop=True)
                    return inst
                P.add("tensor", pfn, [("ktok", h) for h in heads] + [("Vt", c)], [pkk])
                for j, h in enumerate(heads):
                    hb = SM_H + h * 6 * NC
                    gam = sm[:, hb + 3 * NC:hb + 5 * NC]
                    act(Sbf[:, slot[j], :], U[:, h, :], AF.Copy, [("U", h), ("gam", h)], [("Sbf", slot[j])],
                        scale=gam[:, c:c + 1])
                s0 = slot[0]
                sp3 = spsb[:, 0:384].rearrange("p (j t) -> p j t", t=128)
                dve(lambda e, sp3=sp3, s0=s0: e.copy_predicated(
                    out=scT[0:64, s0:s0 + 3, 0:64], mask=trii[0:64, 0:64].unsqueeze(1).to_broadcast([64, 3, 64]),
                    data=sp3[0:64, :, 0:64]), [sk, "trii"], [("scT", r, 0) for r in slot])
                dve(lambda e, sp3=sp3, s0=s0: e.copy_predicated(
                    out=scT[:, s0:s0 + 3, 64:128], mask=trii[:, 64:128].unsqueeze(1).to_broadcast([128, 3, 64]),
                    data=sp3[:, :, 64:128]), [sk, "trii"], [("scT", r, 1) for r in slot])
                gst[gidx] = (heads, slot, c, cs)
                return ppb, pkk

            def seg_U(gidx, ppb, pkk):
                heads, slot, c, cs = gst[gidx]
                for j, h in enumerate(heads):
                    hb = SM_H + h * 6 * NC
                    gam = sm[:, hb + 3 * NC:hb + 5 * NC]
                    dve(lambda e, h=h, c=c, j=j, ppb=ppb, gam=gam: e.scalar_tensor_tensor(
                        out=U[:, h, :], in0=U[:, h, :], scalar=gam[:, c:c + 1], in1=ppb[:, j * 128:(j + 1) * 128],
                        op0=ALU.mult, op1=ALU.add), [("U", h), ("gam", h), pkk], [("U", h)])
                for j, h in enumerate(heads):
                    hb = SM_H + h * 6 * NC
                    gam = sm[:, hb + 3 * NC:hb + 5 * NC]
                    act(U[:, h, :], U[:, h, :], AF.Copy, [("U", h), ("gam", h)], [("U", h)],
                        scale=gam[:, NC + c:NC + c + 1])

            def seg_M(gidx):
                heads, slot, c, cs = gst[gidx]
                opb, ok = bigps()

                def ofn(e, opb=opb, heads=heads, slot=slot, cs=cs, c=c):
                    inst = None
                    for j, h in enumerate(heads):
                        o = opb[:, j * 128:(j + 1) * 128]
                        e.matmul(o, qt[:, h, cs], Sbf[:, slot[j], :], start=True, stop=False)
                        inst = e.matmul(o, scT[:, slot[j], :], Vt[:, c, h * 128:(h + 1) * 128], start=False, stop=True)
                    return inst
                P.add("tensor", ofn, [("qt", h) for h in heads] + [("Sbf", r) for r in slot]
                      + [("scT", r, i) for r in slot for i in range(2)] + [("Vt", c)], [ok])
                q = (gidx % 2) * 3
                ss = sm[:, SM_R + q:SM_R + q + 3]
                ls = sm[:, SM_R + 6 + q:SM_R + 6 + q + 3]
                rs = sm[:, SM_R + 12 + q:SM_R + 12 + q + 3]
                for j in range(3):
                    act(junk[:, q + j, :], opb[:, j * 128:(j + 1) * 128], AF.Square, [ok], [("ss", q + j), ("junk", q + j)], accum_out=ss[:, j:j + 1])
                act(ls, ss, AF.Ln, [("ss", q + j) for j in range(3)], [("ls", q)], scale=1.0 / 128, bias=EPS)
                act(rs, ls, AF.Exp, [("ls", q)], [("rs", q)], scale=-0.5)
                return opb, ok, rs, q

            def seg_L(gidx, opb, ok, rs, q):
                heads, slot, c, cs = gst[gidx]
                s0 = slot[0]
                h0 = heads[0]
                dve(lambda e, opb=opb, s0=s0, rs=rs: e.tensor_tensor(
                    out=onb[:, s0:s0 + 3, :], in0=opb[:, 0:384].rearrange("p (j v) -> p j v", v=128),
                    in1=rs[:, 0:3].unsqueeze(2).to_broadcast([128, 3, 128]), op=ALU.mult),
                    [ok, ("rs", q)], [("onb", r) for r in slot])
                t2b, t2k = bfps()

                def t2fn(e, t2b=t2b, slot=slot):
                    inst = None
                    for j in range(3):
                        inst = e.transpose(t2b[:, j * 128:(j + 1) * 128], onb[:, slot[j], :], identb[:])
                    return inst
                P.add("tensor", t2fn, [("onb", r) for r in slot] + ["identb"], [t2k])
                dve(lambda e, t2b=t2b, h0=h0, cs=cs: e.tensor_tensor(
                    out=headsT[:, h0:h0 + 3, cs], in0=t2b[:, 0:384].rearrange("p (j t) -> p j t", t=128),
                    in1=sgb[:, h0:h0 + 3, cs], op=ALU.mult),
                    [t2k] + [("sgb", h) for h in heads], [("headsT", h) for h in heads])

            nxt = seg_E(0)
            for gidx in range(len(groups)):
                cur = nxt
                m = seg_M(gidx)
                seg_U(gidx, *cur)
                if gidx + 1 < len(groups):
                    nxt = seg_E(gidx + 1)
                seg_L(gidx, *m)
                if inter and gidx % 2 == 1:
                    inter.pop(0)()
            while inter:
                inter.pop(0)()

        def gmlp_layer(inter=()):
            inter = list(inter)
            rmsnorm_fm("mixn1")
            w, wkey = get_weight("l1_v")
            w3 = w[:, 0:KD * 768].rearrange("p (k c) -> p k c", c=768)
            for c in range(NC):
                for half in range(2):
                    ps, pk = bigps()
                    mm_group(ps[:, 0:384], [(hT[:, kc, c * 128:(c + 1) * 128], w3[:, kc, half * 384:(half + 1) * 384])
                                            for kc in range(KD)], [wkey] + hT_keys, [pk])
                    act(vg[:, c, half * 384:(half + 1) * 384], ps[:, 0:384], AF.Gelu, [pk], vgk(c, half))
                    dve(lambda e, c=c, half=half: e.bn_stats(out=st6[:, c, half * 6:(half + 1) * 6],
                                                             in_=vg[:, c, half * 384:(half + 1) * 384]),
                        vgk(c, half), [("st6", c, half)])
                dve(lambda e, c=c: e.bn_aggr(out=mv[:, c, :], in_=st6[:, c, :]),
                    [("st6", c, 0), ("st6", c, 1)], [("mv", c)])
            for a in range(2):
                w, wkey = get_weight("l1_u%d" % a)
                w3 = w[:, 0:KD * 512].rearrange("p (k c) -> p k c", c=512)
                for bq in range(4):
                    ps, pk = bigps()
                    mm_group(ps[:], [(w3[:, kc, bq * 128:(bq + 1) * 128], hT[:, kc, :]) for kc in range(KD)],
                             [wkey] + hT_keys, [pk])
                    blk = a * 4 + bq
                    if blk < 6:
                        act(uT[:, blk, :], ps[:], AF.Gelu, [pk], [("uT", blk), "uT", UTA[blk]])
                    else:
                        act(qmT[:, blk - 6, :], ps[:], AF.Copy, [pk], [("qmT", blk - 6)])
            lv = sm[:, SM_L:SM_L + NC]
            rv = sm[:, SM_L + NC:SM_L + 2 * NC]
            act(lv, mv[:, :, 1], AF.Ln, [("mv", c) for c in range(NC)], ["lv"], bias=EPS)
            act(rv, lv, AF.Exp, ["lv"], ["rv"], scale=-0.5)
            for c in range(NC):
                cs = slice(c * 128, (c + 1) * 128)
                dve(lambda e, c=c: e.tensor_scalar(out=Vt[:, c, :], in0=vg[:, c, :], scalar1=mv[:, c, 0:1],
                                                   scalar2=rv[:, c:c + 1], op0=ALU.subtract, op1=ALU.mult),
                    vgk(c, 0) + vgk(c, 1) + [("mv", c), "rv"], [("Vt", c)])
                for gh in range(2):
                    spb, sk = bigps()

                    def sfn(e, spb=spb, gh=gh, c=c):
                        inst = None
                        for gi in range(3):
                            g = gh * 3 + gi
                            inst = e.matmul(spb[:, gi * 128:(gi + 1) * 128], Vt[:, c, g * 128:(g + 1) * 128], wsb[:, g, :],
                                            start=True, stop=True)
                        return inst
                    P.add("tensor", sfn, [("Vt", c)] + [("wsb", g) for g in range(NH)], [sk])
                    r0 = rot("gmb", 2) * 3
                    for gi in range(3):
                        g = gh * 3 + gi
                        sps = spb[:, gi * 128:(gi + 1) * 128]
                        dve(lambda e, sps=sps, g=g, r=r0 + gi: e.scalar_tensor_tensor(
                            out=svt[:, r, :], in0=sps, scalar=pv[:, PV["lng"] + g:PV["lng"] + g + 1], in1=gbias[:, g, :],
                            op0=ALU.mult, op1=ALU.add), [sk, "pv", ("gbias", g)], [("svt", r0 + gi)])
                    g0 = gh * 3
                    dve(lambda e, g0=g0, r0=r0, cs=cs: e.tensor_tensor(out=headsT[:, g0:g0 + 3, cs], in0=svt[:, r0:r0 + 3, :],
                                                                     in1=uT[:, g0:g0 + 3, cs], op=ALU.mult),
                        [("svt", r0 + i) for i in range(3)] + [("uT", g0 + i) for i in range(3)] + [UTA[g0 + i] for i in range(3)],
                        [("headsT", g0 + i) for i in range(3)])
                if inter:
                    inter.pop(0)()
            while inter:
                inter.pop(0)()

        SQ2SL = [6, 7, 10, 11]

        def sq2(kc):
            return hgA[:, SQ2SL[kc // 2], :].bitcast(BF16)[:, (kc % 2) * T:(kc % 2 + 1) * T]

        def sq2k(kc):
            sl = kc // 2
            return [("E1", sl, c) for c in range(NC)] if sl < 2 else [("qsb", sl - 2)]

        def pre_rmsnorm_next(nsel):
            Xn = xbufs[nsel]
            for kc in range(KD):
                act(sq2(kc), Xn[:, kc, :], AF.Square, [("xT%d" % nsel, kc)], sq2k(kc))
            ps, pk = bigps()
            mm_group(ps[:], [(onesb[:], sq2(kc)) for kc in range(KD)],
                     ["onesb"] + [k for kc in range(0, KD, 2) for k in sq2k(kc)], [pk])
            act(lnv[:], ps[:], AF.Ln, [pk], ["lnv"], scale=1.0 / D, bias=EPS)
            act(rstd2[:], lnv[:], AF.Exp, ["lnv"], ["rstd"], scale=-0.5)
            for kc in range(KD):
                dve(lambda e, kc=kc, Xn=Xn: e.scalar_tensor_tensor(out=hT[:, kc, :], in0=Xn[:, kc, :],
                                                                   scalar=pv[:, PV["mixn0"] + kc:PV["mixn0"] + kc + 1],
                                                                   in1=rstd2[:], op0=ALU.mult, op1=ALU.mult),
                    [("xT%d" % nsel, kc), "pv", "rstd"], [("hT", kc)])

        def load_x(tt, sel):
            for kc in range(KD):
                P.add("sync", lambda e, tt=tt, kc=kc, sel=sel: e.dma_start(out=xbufs[sel][:, kc, :],
                                                                         in_=xT_v[:, kc, tt * T:(tt + 1) * T]),
                      writes=[("xT%d" % sel, kc)], dma=True, semkey="xT%d_%d" % (sel, kc))

        fstage = [actT[:, 8 + 2 * i:10 + 2 * i, :].rearrange("p a b -> p (a b)").bitcast(F32) for i in range(7)] + [sgt[:, 0, :]]
        FKEYS = [[("actT", 8 + 2 * i), ("actT", 9 + 2 * i)] for i in range(7)] + [[("sgt", 0)]]

        def final_norm(sel, t0):
            X = xbufs[sel]
            if final:
                for kc in range(KD):
                    act(sq[:, kc, :], X[:, kc, :], AF.Square, [("xT%d" % sel, kc)], [SQK[kc]])
                ps, pk = bigps()
                mm_group(ps[:], [(onesb[:], sq[:, kc, :]) for kc in range(KD)], ["onesb"] + SQK, [pk])
                act(lnv[:], ps[:], AF.Ln, [pk], ["lnv"], scale=1.0 / D, bias=EPS)
                act(rstd[:], lnv[:], AF.Exp, ["lnv"], ["rstd"], scale=-0.5)
                for kc in range(KD):
                    dve(lambda e, kc=kc, X=X: e.scalar_tensor_tensor(
                        out=fstage[kc], in0=X[:, kc, :], scalar=pv[:, PV["finn"] + kc:PV["finn"] + kc + 1],
                        in1=rstd[:], op0=ALU.mult, op1=ALU.mult), [("xT%d" % sel, kc), "pv", "rstd"], FKEYS[kc])
                    P.add("sync", lambda e, kc=kc, t0=t0: e.dma_start(out=outT_v[:, kc, t0:t0 + T], in_=fstage[kc]),
                          reads=FKEYS[kc], dma=True, semkey="out%d" % kc)
            else:
                for kc in range(KD):
                    P.add("sync", lambda e, kc=kc, t0=t0, X=X: e.dma_start(out=outT_v[:, kc, t0:t0 + T], in_=X[:, kc, :]),
                          reads=[("xT%d" % sel, kc)], dma=True, semkey="outx%d" % kc)

        pending = []
        for ti in range(ntiles):
            t0 = ti * T
            sel = ti % 2
            xsel[0] = sel
            if ti == 0:
                load_x(0, 0)

            def after_iv(ti=ti):
                while pending:
                    pending.pop(0)()
                if ti + 1 < ntiles:
                    load_x(ti + 1, (ti + 1) % 2)

            pre = (ti > 0) and (0 in layers) and (1 in layers)
            if 0 in layers:
                hgrn_layer(mem_attn_parts(0), pre_done=pre, after_iv=after_iv)
                out_proj(0)
                ffn(0)
            else:
                after_iv()
                for n in tile_names:
                    if n.startswith("l0"):
                        get_weight(n)
            if 1 in layers:
                gmlp_layer(mem_attn_parts(1))
                out_proj(1)
                nxt_pre = (ti + 1 < ntiles) and (0 in layers)
                ffn(1, mid=(lambda ti=ti: pre_rmsnorm_next((ti + 1) % 2)) if nxt_pre else None)
            else:
                for n in tile_names:
                    if n.startswith("l1"):
                        get_weight(n)
            pending.append(lambda sel=sel, t0=t0: final_norm(sel, t0))
        while pending:
            pending.pop(0)()
        P.emit(nc, st)
    return nc


_CACHE = {}


def kernel(**inputs):
    shared, per_core, offs, setup_names, tile_names = _pack_host(inputs)
    ftot = shared["wall"].shape[1]
    nc = build_program(offs, setup_names, tile_names, ftot)
    in_maps = []
    for b in range(8):
        m = dict(shared)
        m.update(per_core[b])
        in_maps.append(m)
    res = run_bass_kernel_spmd(nc, in_maps, core_ids=list(range(8)))
    out = np.stack([np.ascontiguousarray(res.results[b]["outT"].T) for b in range(8)], axis=0)
    return out.astype(np.float32)
```
